# Optimizing a Trainium2 kernel written in Bass

```python
import jax, jax.numpy as jnp
from jax import lax
import numpy as np

D_MODEL = 2048
BATCH = 2
SEQ = 4096
DEPTH = 1

CHUNK = 64
N_MEM = 256
MIX_WIDTH = D_MODEL
CONV_CH = MIX_WIDTH // 2
CONV_WIDTH = 31
SGU_CH = MIX_WIDTH - CONV_CH
SGU_GROUPS = 8
SGU_GROUP_DIM = SGU_CH // SGU_GROUPS
GMLP_CHUNK = 128
XA_HEADS = 4
XA_HEAD_DIM = D_MODEL // XA_HEADS
D_FF = 5632
MACARON_SCALE = 0.5
RMS_EPS = 1e-6
LN_EPS = 1e-5

kernel_name = "hybrid_conformer_conv_gmlp_macaron"


def rms_norm(x, g):
    xf = x.astype(jnp.float32)
    y = xf * lax.rsqrt(jnp.mean(xf * xf, axis=-1, keepdims=True) + RMS_EPS)
    return (y * g.astype(jnp.float32)).astype(x.dtype)


def layer_norm(x, g, b):
    xf = x.astype(jnp.float32)
    mu = jnp.mean(xf, axis=-1, keepdims=True)
    xc = xf - mu
    y = xc * lax.rsqrt(jnp.mean(xc * xc, axis=-1, keepdims=True) + LN_EPS)
    return (y * g.astype(jnp.float32) + b.astype(jnp.float32)).astype(x.dtype)


def swiglu_ffn(h, w_in, w_out):
    gate, up = jnp.split(h @ w_in, 2, axis=-1)
    return (jax.nn.silu(gate) * up) @ w_out


def conv_module_group(val, gate, conv_w, conv_b, ln_g, ln_b):
    a = val * jax.nn.sigmoid(gate)
    y = lax.conv_general_dilated(
        a, conv_w[:, None, :],
        window_strides=(1,),
        padding=[(CONV_WIDTH - 1, 0)],
        dimension_numbers=("NWC", "WIO", "NWC"),
        feature_group_count=CONV_CH)
    y = y + conv_b
    y = layer_norm(y, ln_g, ln_b)
    return jax.nn.silu(y)


def spatial_gating_group(u, v, ln_g, ln_b, w_s, b_s):
    B, S, _ = u.shape
    n_chunks = S // GMLP_CHUNK
    v = layer_norm(v, ln_g, ln_b)
    blk = jnp.arange(GMLP_CHUNK) // CHUNK
    mask = blk[None, :] <= blk[:, None]
    w = jnp.where(mask[None], w_s, jnp.zeros((), w_s.dtype))
    vh = v.reshape(B, n_chunks, GMLP_CHUNK, SGU_GROUPS, SGU_GROUP_DIM)
    mixed = jnp.einsum("hij,bnjhc->bnihc", w, vh)
    mixed = mixed + jnp.transpose(b_s)[None, None, :, :, None]
    return u * mixed.reshape(B, S, SGU_CH)


def memory_cross_attention(h, mem_n, w_q, w_kv, w_o):
    B, S, _ = h.shape
    M = mem_n.shape[1]
    q = (h @ w_q).reshape(B, S, XA_HEADS, XA_HEAD_DIM)
    k, v = jnp.split(mem_n @ w_kv, 2, axis=-1)
    k = k.reshape(B, M, XA_HEADS, XA_HEAD_DIM)
    v = v.reshape(B, M, XA_HEADS, XA_HEAD_DIM)
    s = jnp.einsum("bshd,bmhd->bhsm", q, k).astype(jnp.float32) * (XA_HEAD_DIM ** -0.5)
    p = jax.nn.softmax(s, axis=-1).astype(v.dtype)
    o = jnp.einsum("bhsm,bmhd->bshd", p, v).reshape(B, S, D_MODEL)
    return o @ w_o


def setup_inputs(seed: int = 0) -> dict:
    key = jax.random.key(seed)
    ks = jax.random.split(key, 32)
    L, D, F = DEPTH, D_MODEL, D_FF

    def nrm(k, shape, fan_in):
        return jax.random.normal(k, shape, jnp.float32) * (fan_in ** -0.5)

    def gain(k, shape):
        return 1.0 + 0.05 * jax.random.normal(k, shape, jnp.float32)

    def bias(k, shape, s=0.02):
        return s * jax.random.normal(k, shape, jnp.float32)

    return {
        "x": jax.random.normal(ks[0], (BATCH, SEQ, D), jnp.float32),
        "mem": jax.random.normal(ks[1], (BATCH, N_MEM, D), jnp.float32),
        "ffn1_norm": gain(ks[2], (L, D)),
        "ffn1_w_in": nrm(ks[3], (L, D, 2 * F), D),
        "ffn1_w_out": nrm(ks[4], (L, F, D), F),
        "mix_norm": gain(ks[5], (L, D)),
        "w_mix_in": nrm(ks[6], (L, D, 2 * CONV_CH + 2 * SGU_CH), D),
        "conv_w": nrm(ks[7], (L, CONV_WIDTH, CONV_CH), CONV_WIDTH),
        "conv_b": bias(ks[8], (L, CONV_CH)),
        "conv_ln_g": gain(ks[9], (L, CONV_CH)),
        "conv_ln_b": bias(ks[10], (L, CONV_CH)),
        "sgu_ln_g": gain(ks[11], (L, SGU_CH)),
        "sgu_ln_b": bias(ks[12], (L, SGU_CH)),
        "sgu_w": nrm(ks[13], (L, SGU_GROUPS, GMLP_CHUNK, GMLP_CHUNK), GMLP_CHUNK),
        "sgu_b": 1.0 + 0.1 * jax.random.normal(ks[14], (L, SGU_GROUPS, GMLP_CHUNK), jnp.float32),
        "out_norm_conv": gain(ks[15], (L, CONV_CH)),
        "out_norm_sgu": gain(ks[16], (L, SGU_CH)),
        "w_mix_out": nrm(ks[17], (L, MIX_WIDTH, D), MIX_WIDTH),
        "xattn_norm": gain(ks[18], (L, D)),
        "mem_norm": gain(ks[19], (L, D)),
        "w_q": nrm(ks[20], (L, D, D), D),
        "w_kv": nrm(ks[21], (L, D, 2 * D), D),
        "w_o": nrm(ks[22], (L, D, D), D),
        "ffn2_norm": gain(ks[23], (L, D)),
        "ffn2_w_in": nrm(ks[24], (L, D, 2 * F), D),
        "ffn2_w_out": nrm(ks[25], (L, F, D), F),
        "final_norm": gain(ks[26], (D,)),
    }


def reference(x, mem, ffn1_norm, ffn1_w_in, ffn1_w_out, mix_norm, w_mix_in,
              conv_w, conv_b, conv_ln_g, conv_ln_b, sgu_ln_g, sgu_ln_b, sgu_w, sgu_b,
              out_norm_conv, out_norm_sgu, w_mix_out, xattn_norm, mem_norm,
              w_q, w_kv, w_o, ffn2_norm, ffn2_w_in, ffn2_w_out, final_norm):
    split_at = [CONV_CH, 2 * CONV_CH, 2 * CONV_CH + SGU_CH]
    for l in range(DEPTH):
        h = rms_norm(x, ffn1_norm[l])
        x = x + MACARON_SCALE * swiglu_ffn(h, ffn1_w_in[l], ffn1_w_out[l])

        h = rms_norm(x, mix_norm[l])
        p = h @ w_mix_in[l]
        a_val, a_gate, g_u, g_v = jnp.split(p, split_at, axis=-1)
        ya = conv_module_group(a_val, a_gate, conv_w[l], conv_b[l], conv_ln_g[l], conv_ln_b[l])
        yb = spatial_gating_group(jax.nn.gelu(g_u, approximate=False),
                                  jax.nn.gelu(g_v, approximate=False),
                                  sgu_ln_g[l], sgu_ln_b[l], sgu_w[l], sgu_b[l])
        y = jnp.concatenate([rms_norm(ya, out_norm_conv[l]),
                             rms_norm(yb, out_norm_sgu[l])], axis=-1)
        x = x + y @ w_mix_out[l]

        h = rms_norm(x, xattn_norm[l])
        mem_n = rms_norm(mem, mem_norm[l])
        x = x + memory_cross_attention(h, mem_n, w_q[l], w_kv[l], w_o[l])

        h = rms_norm(x, ffn2_norm[l])
        x = x + MACARON_SCALE * swiglu_ffn(h, ffn2_w_in[l], ffn2_w_out[l])
    return rms_norm(x, final_norm)
```

```python
import contextlib
import os
import numpy as np
import concourse.bass as bass
import concourse.mybir as mybir
from concourse.bass_utils import run_bass_kernel_spmd

F32 = mybir.dt.float32
BF16 = mybir.dt.bfloat16
I32 = mybir.dt.int32
AF = mybir.ActivationFunctionType
ALU = mybir.AluOpType
AX = mybir.AxisListType

D = 2048
DFF = 5632
KD = D // 128
NF = DFF // 128
SEQ = 4096
TOK = 1024
HALO = 32
TE = TOK + HALO
NMEM = 256
CW = 31
RMS_EPS = 1e-6
LN_EPS = 1e-5
FF_GROUPS = [(0, 12), (12, 24), (24, 34), (34, 44)]
NT_MIX = 4
TQ = TOK // NT_MIX
SLOT_ELEMS = 8192
NSLOT = 2

C_FFN1, C_MIX, C_XATT, C_MEM, C_FFN2, C_FINAL = 0, 16, 32, 48, 64, 80
C_CONVB, C_CLNG, C_CLNB, C_ONC, C_ONS, C_CONVW = 96, 104, 112, 120, 128, 136
NCOLS = C_CONVW + 8 * CW


ENGINES = ("pe", "act", "dve", "pool", "sp")


class _Op:
    __slots__ = ("fn", "waits", "ticket", "inc")


class Sched:
    def __init__(self):
        self.ops = {e: [] for e in ENGINES}
        self.count = {}
        self.last_writer = {}
        self.readers = {}
        self.waited = {e: {} for e in ENGINES}
        self.barrier_t = []

    def barrier(self):
        self.barrier_t = [(e, self.count.get(e, 0)) for e in ("pe", "act", "dve")]

    def add(self, eng, fn, reads=(), writes=(), dma_sem=None, nobarrier=False):
        deps = {}

        def need(t):
            if t is not None and deps.get(t[0], 0) < t[1]:
                deps[t[0]] = t[1]

        for key in reads:
            need(self.last_writer.get(key))
        for key in writes:
            need(self.last_writer.get(key))
            for t in self.readers.get(key, ()):
                need(t)
        if not nobarrier:
            for t in self.barrier_t:
                need(t)
        if dma_sem is not None:
            semkey, inc = dma_sem, 16
        else:
            semkey, inc = eng, 1
        val = self.count.get(semkey, 0) + inc
        self.count[semkey] = val
        ticket = (semkey, val)
        waits = []
        w = self.waited[eng]
        for k, v in deps.items():
            if v <= 0 or w.get(k, 0) >= v:
                continue
            w[k] = v
            waits.append((k, v))
        op = _Op()
        op.fn, op.waits, op.ticket, op.inc = fn, waits, ticket, inc
        self.ops[eng].append(op)
        for key in writes:
            self.last_writer[key] = ticket
            self.readers[key] = []
        for key in reads:
            if key not in writes:
                self.readers.setdefault(key, []).append(ticket)
        return ticket

    def emit(self, nc, final_waits=()):
        with contextlib.ExitStack() as st:
            sems = {k: st.enter_context(nc.semaphore("s_" + str(k))) for k in self.count}
            block = st.enter_context(nc.Block())

            def run(engname):
                def body(eng):
                    for op in self.ops[engname]:
                        for k, v in op.waits:
                            eng.wait_ge(sems[k], v)
                        ins = op.fn(eng)
                        ins.then_inc(sems[op.ticket[0]], op.inc)
                    if engname == "sp":
                        for k, v in final_waits:
                            eng.wait_ge(sems[k], v)
                return body

            block.tensor(run("pe"))
            block.scalar(run("act"))
            block.vector(run("dve"))
            block.gpsimd(run("pool"))
            block.sync(run("sp"))


def _blk(W, rows, cols):
    kc, c = len(rows), len(cols)
    ridx = (np.asarray(rows)[:, None] * 128 + np.arange(128)[None, :]).reshape(-1)
    sub = W[ridx][:, cols]
    return np.ascontiguousarray(sub.reshape(kc, 128, c).transpose(1, 0, 2).reshape(128, kc * c))


def _ffn_blocks(w_in, w_out):
    out = []
    ar = np.arange(128)
    allk = list(range(KD))
    for (g0, g1) in FF_GROUPS:
        for j0 in range(g0, g1, 2):
            cols = np.concatenate([j0 * 128 + ar, DFF + j0 * 128 + ar,
                                   (j0 + 1) * 128 + ar, DFF + (j0 + 1) * 128 + ar])
            out.append(_blk(w_in, allk, cols))
        for cb in range(4):
            out.append(_blk(w_out, list(range(g0, g1)), cb * 512 + np.arange(512)))
    return out


def _build_wstream(inp):
    blocks = []
    ar = np.arange(128)
    a512 = np.arange(512)
    allk = list(range(KD))
    blocks += _ffn_blocks(inp["ffn1_w_in"][0], inp["ffn1_w_out"][0])
    wmi = inp["w_mix_in"][0]
    mix = []
    for i in range(4):
        c0, c1 = 2 * i, 2 * i + 1
        cols = np.concatenate([c0 * 128 + ar, 1024 + c0 * 128 + ar, c1 * 128 + ar, 1024 + c1 * 128 + ar])
        mix.append(_blk(wmi, allk, cols))
    for ub in range(2):
        mix.append(_blk(wmi, allk, 2048 + ub * 512 + a512))
    for vb in range(2):
        mix.append(_blk(wmi, allk, 3072 + vb * 512 + a512))
    wmo = inp["w_mix_out"][0]
    for cb in range(4):
        mix.append(_blk(wmo, allk, cb * 512 + a512))
    blocks += mix
    wkv = inp["w_kv"][0]
    for kb in range(4):
        blocks.append(_blk(wkv, allk, kb * 512 + a512))
    for vb in range(4):
        blocks.append(_blk(wkv, allk, 2048 + vb * 512 + a512))
    for qb in range(4):
        blocks.append(_blk(inp["w_q"][0], allk, qb * 512 + a512))
    for ob in range(4):
        blocks.append(_blk(inp["w_o"][0], allk, ob * 512 + a512))
    blocks += _ffn_blocks(inp["ffn2_w_in"][0], inp["ffn2_w_out"][0])
    offs = np.cumsum([0] + [b.shape[1] for b in blocks])
    return np.concatenate(blocks, axis=1), offs


def _wstream_layout():
    ffn = []
    for (g0, g1) in FF_GROUPS:
        ffn += [KD * 512] * ((g1 - g0) // 2)
        ffn += [(g1 - g0) * 512] * 4
    lens = ffn + [KD * 512] * 12 + [KD * 512] * 16 + ffn
    offs = np.cumsum([0] + lens)
    return lens, offs


def _col(v):
    return np.asarray(v, np.float32).reshape(-1, 128).T


def _build_cols(inp):
    cols = np.zeros((128, NCOLS), np.float32)
    cols[:, C_FFN1:C_FFN1 + 16] = _col(inp["ffn1_norm"][0])
    cols[:, C_MIX:C_MIX + 16] = _col(inp["mix_norm"][0])
    cols[:, C_XATT:C_XATT + 16] = _col(inp["xattn_norm"][0])
    cols[:, C_MEM:C_MEM + 16] = _col(inp["mem_norm"][0])
    cols[:, C_FFN2:C_FFN2 + 16] = _col(inp["ffn2_norm"][0])
    cols[:, C_FINAL:C_FINAL + 16] = _col(inp["final_norm"])
    cols[:, C_CONVB:C_CONVB + 8] = _col(inp["conv_b"][0])
    cols[:, C_CLNG:C_CLNG + 8] = _col(inp["conv_ln_g"][0])
    cols[:, C_CLNB:C_CLNB + 8] = _col(inp["conv_ln_b"][0])
    cols[:, C_ONC:C_ONC + 8] = _col(inp["out_norm_conv"][0])
    cols[:, C_ONS:C_ONS + 8] = _col(inp["out_norm_sgu"][0])
    cw = np.asarray(inp["conv_w"][0], np.float32)
    cols[:, C_CONVW:] = cw.T.reshape(8, 128, CW).transpose(1, 0, 2).reshape(128, 8 * CW)
    return cols


def build_program(stop_after=None):
    nc = bass.Bass("TRN2", target_bir_lowering=False)
    lens, offs = _wstream_layout()
    WTOT = int(offs[-1])
    xT_d = nc.dram_tensor("xT", [D, TE], F32, kind="ExternalInput").ap()
    memT_d = nc.dram_tensor("memT", [D, NMEM], F32, kind="ExternalInput").ap()
    w_d = nc.dram_tensor("wstream", [128, WTOT], F32, kind="ExternalInput").ap()
    cols_d = nc.dram_tensor("cols", [128, NCOLS], F32, kind="ExternalInput").ap()
    rows_d = nc.dram_tensor("rows", [1, 3072], F32, kind="ExternalInput").ap()
    wsT_d = nc.dram_tensor("wsT", [128, 8 * 128], F32, kind="ExternalInput").ap()
    out_d = nc.dram_tensor("outT", [D, TOK], F32, kind="ExternalOutput").ap()

    st = contextlib.ExitStack()
    with st:
        def SB(name, shape, dt):
            return st.enter_context(nc.sbuf_tensor(name, shape, dt))

        x32 = SB("x32", [128, KD, TE], F32)
        cols = SB("cols_sb", [128, NCOLS], F32)
        ones_bf = SB("ones_bf", [128, 128], BF16)
        ident_bf = SB("ident_bf", [128, 128], BF16)
        wmT = SB("wmT", [128, 8, 128], BF16)
        consts = SB("consts", [128, 4], F32)
        stat = SB("stat", [128, 64], F32)
        wslots = [SB(f"wslot{i}", [128, SLOT_ELEMS], BF16) for i in range(NSLOT)]
        ARENA_F32 = 26368
        arena = SB("arena", [128, ARENA_F32], F32)
        psum = st.enter_context(nc.psum_tensor("psum", [128, 8, 512], F32))

        S = Sched()
        eps_rms = consts[:, 0:1]
        eps_ln = consts[:, 1:2]

        class Arena:
            def __init__(self):
                self.off = 0

            def f32(self, n):
                a = arena[:, self.off:self.off + n]
                self.off += n
                assert self.off <= ARENA_F32, self.off
                return a

            def bf16(self, n):
                assert n % 2 == 0
                a = arena[:, self.off:self.off + n // 2].bitcast(BF16)
                self.off += n // 2
                assert self.off <= ARENA_F32, self.off
                return a

        bank_ctr = [0]

        def banks(n):
            r = [(bank_ctr[0] + i) % 8 for i in range(n)]
            bank_ctr[0] += n
            return r

        def pkey(b):
            return ("ps", b)

        wcount = [0]

        def wload(off, L):
            slot = wcount[0] % NSLOT
            wcount[0] += 1
            key = ("w", slot)
            S.add("pool", lambda e, slot=slot, off=off, L=L: e.dma_start(
                out=wslots[slot][:, 0:L], in_=w_d[:, off:off + L]),
                writes=[key], dma_sem=f"wsem{slot}", nobarrier=True)
            return wslots[slot], key

        items = []

        def rms_feat(tag, srcs, gcol0, dsts, T, tiles, nfeat, sq, rstd, eps=eps_rms):
            n = len(srcs)
            bs = banks(len(tiles))
            for k, (sap, skey) in enumerate(srcs):
                sl = k % 2
                S.add("act", lambda e, sap=sap, sl=sl: e.activation(sq[sl][:, 0:T], sap, AF.Square),
                      reads=[skey], writes=[("sqbuf", sl)])

                def mm(e, k=k, sl=sl):
                    ins = None
                    for ti, (t0, nn) in enumerate(tiles):
                        ins = e.matmul(psum[:, bs[ti], 0:nn], ones_bf[:], sq[sl][:, t0:t0 + nn],
                                       start=(k == 0), stop=(k == n - 1))
                    return ins
                S.add("pe", mm, reads=[("sqbuf", sl)], writes=[pkey(b) for b in bs])

            def sq_(e):
                ins = None
                for ti, (t0, nn) in enumerate(tiles):
                    ins = e.activation(rstd[:, t0:t0 + nn], psum[:, bs[ti], 0:nn], AF.Sqrt,
                                       bias=eps, scale=1.0 / nfeat)
                return ins
            S.add("act", sq_, reads=[pkey(b) for b in bs] + [("c0",), ("c1",)], writes=[("rstdbuf",)])
            S.add("dve", lambda e: e.reciprocal(rstd[:, 0:T], rstd[:, 0:T]),
                  reads=[("rstdbuf",)], writes=[("rstdbuf",)])
            for k, ((sap, skey), (dap, dkey)) in enumerate(zip(srcs, dsts)):
                S.add("dve", lambda e, sap=sap, dap=dap, k=k: e.scalar_tensor_tensor(
                    out=dap, in0=sap, scalar=cols[:, gcol0 + k:gcol0 + k + 1], in1=rstd[:, 0:T],
                    op0=ALU.mult, op1=ALU.mult),
                    reads=[skey, ("rstdbuf",), ("cols",)] + ([dkey] if dkey != skey else []), writes=[dkey])

        def rms_split(srcs, gcol0, dsts, T, tiles, nfeat, sq, rstd, eps=eps_rms, rkey=("rstdbuf",)):
            n = len(srcs)
            bs = banks(len(tiles))
            for k, ((sap, skey), (dap, dkey)) in enumerate(zip(srcs, dsts)):
                sl = k % 2
                S.add("act", lambda e, sap=sap, dap=dap, k=k: e.activation(dap, sap, AF.Identity,
                                                                          scale=cols[:, gcol0 + k:gcol0 + k + 1]),
                      reads=[skey, ("cols",)], writes=[dkey])
                S.add("dve", lambda e, sap=sap, sl=sl: e.tensor_tensor(out=sq[sl][:, 0:T], in0=sap, in1=sap, op=ALU.mult),
                      reads=[skey], writes=[("sqbuf", sl)])

                def mm(e, k=k, sl=sl):
                    ins = None
                    for ti, (t0, nn) in enumerate(tiles):
                        ins = e.matmul(psum[:, bs[ti], 0:nn], ones_bf[:], sq[sl][:, t0:t0 + nn],
                                       start=(k == 0), stop=(k == n - 1))
                    return ins
                S.add("pe", mm, reads=[("sqbuf", sl), ("ones",)], writes=[pkey(b) for b in bs])

            def sq_(e):
                ins = None
                for ti, (t0, nn) in enumerate(tiles):
                    ins = e.activation(rstd[:, t0:t0 + nn], psum[:, bs[ti], 0:nn], AF.Sqrt, bias=eps, scale=1.0 / nfeat)
                return ins
            S.add("act", sq_, reads=[pkey(b) for b in bs] + [("c0",), ("c1",)], writes=[rkey])
            S.add("dve", lambda e: e.reciprocal(rstd[:, 0:T], rstd[:, 0:T]), reads=[rkey], writes=[rkey])

        def mm_group(wv, wkey, kc, C, co, rhs, tiles, bs):
            def mm(e):
                ins = None
                for k in range(kc):
                    lw = wv[:, k * C + co:k * C + co + 128]
                    for ti, (t0, nn) in enumerate(tiles):
                        ins = e.matmul(psum[:, bs[ti], 0:nn], lw, rhs[k][0][:, t0:t0 + nn],
                                       start=(k == 0), stop=(k == kc - 1))
                return ins
            S.add("pe", mm, reads=[wkey] + [r[1] for r in rhs[:kc]], writes=[pkey(b) for b in bs])

        xT_v = xT_d.rearrange("(k p) t -> p k t", p=128)
        for k in range(KD):
            S.add("sp", lambda e, k=k: e.dma_start(out=x32[:, k, :], in_=xT_v[:, k, :]),
                  writes=[("x", k)], dma_sem=f"xin{k}")
        S.add("sp", lambda e: e.dma_start(out=cols[:], in_=cols_d), writes=[("cols",)], dma_sem="cin")
        S.add("dve", lambda e: e.memset(ones_bf[:], 1.0), writes=[("ones",)])
        S.add("dve", lambda e: e.memset(consts[:, 0:1], RMS_EPS), writes=[("c0",)])
        S.add("dve", lambda e: e.memset(consts[:, 1:2], LN_EPS), writes=[("c1",)])
        XK = [("x", k) for k in range(KD)]

        def ffn_phase(tag, gcol0, blk0, t_lo, T, tiles):
            ar = Arena()
            hb = ar.bf16(KD * T)
            gb = ar.bf16(12 * T)
            sq = [ar.bf16(T + (T % 2)) for _ in range(2)]
            sg = [ar.f32(T) for _ in range(2)]
            un = [ar.f32(T) for _ in range(2)]
            rstd = ar.f32(T)
            hbk = [(hb[:, k * T:(k + 1) * T], (tag + "hb", k)) for k in range(KD)]
            gbk = [(gb[:, k * T:(k + 1) * T], (tag + "gb", k)) for k in range(12)]
            xs = [(x32[:, k, t_lo:t_lo + T], ("x", k)) for k in range(KD)]

            def pre(ws, wk):
                S.barrier()
                rms_split(xs, gcol0, hbk, T, tiles, D, sq, rstd)
            items.append((None, pre))
            bi = blk0
            sgc = [0]
            for (g0, g1) in FF_GROUPS:
                for j0 in range(g0, g1, 2):
                    def win(ws, wk, j0=j0, g0=g0):
                        for jj in range(2):
                            j = j0 + jj
                            bg = banks(len(tiles))
                            bu = banks(len(tiles))
                            mm_group(ws, wk, KD, 512, jj * 256, hbk, tiles, bg)
                            mm_group(ws, wk, KD, 512, jj * 256 + 128, hbk, tiles, bu)
                            sl = sgc[0] % 2
                            sgc[0] += 1

                            def gn(e, bg=bg, sl=sl):
                                ins = None
                                for ti, (t0, nn) in enumerate(tiles):
                                    ins = e.tensor_tensor(out=sg[sl][:, t0:t0 + nn], in0=psum[:, bg[ti], 0:nn],
                                                          in1=rstd[:, t0:t0 + nn], op=ALU.mult)
                                return ins
                            S.add("dve", gn, reads=[pkey(b) for b in bg] + [("rstdbuf",)], writes=[(tag + "sg", sl)])
                            S.add("act", lambda e, sl=sl: e.activation(sg[sl][:, 0:T], sg[sl][:, 0:T], AF.Silu),
                                  reads=[(tag + "sg", sl)], writes=[(tag + "sg", sl)])

                            def upn(e, bu=bu, sl=sl):
                                ins = None
                                for ti, (t0, nn) in enumerate(tiles):
                                    ins = e.tensor_tensor(out=un[sl][:, t0:t0 + nn], in0=psum[:, bu[ti], 0:nn],
                                                          in1=rstd[:, t0:t0 + nn], op=ALU.mult)
                                return ins
                            S.add("dve", upn, reads=[pkey(b) for b in bu] + [("rstdbuf",)], writes=[(tag + "un", sl)])
                            S.add("dve", lambda e, sl=sl, j=j, g0=g0: e.tensor_tensor(
                                out=gbk[j - g0][0], in0=sg[sl][:, 0:T], in1=un[sl][:, 0:T], op=ALU.mult),
                                reads=[(tag + "sg", sl), (tag + "un", sl)], writes=[gbk[j - g0][1]])
                    items.append((bi, win))
                    bi += 1
                kc = g1 - g0
                for cb in range(4):
                    def wout(ws, wk, cb=cb, kc=kc):
                        for mmi in range(4):
                            m = cb * 4 + mmi
                            bo = banks(len(tiles))
                            mm_group(ws, wk, kc, 512, mmi * 128, gbk, tiles, bo)

                            def acc(e, bo=bo, m=m):
                                ins = None
                                for ti, (t0, nn) in enumerate(tiles):
                                    xa = x32[:, m, t_lo + t0:t_lo + t0 + nn]
                                    ins = e.scalar_tensor_tensor(out=xa, in0=psum[:, bo[ti], 0:nn], scalar=0.5,
                                                                 in1=xa, op0=ALU.mult, op1=ALU.add)
                                return ins
                            S.add("dve", acc, reads=[pkey(b) for b in bo] + [("x", m)], writes=[("x", m)])
                    items.append((bi, wout))
                    bi += 1
            return bi

        nblk_ffn = sum((g1 - g0) // 2 + 4 for g0, g1 in FF_GROUPS)
        B_FFN1 = 0
        B_MIX = nblk_ffn
        B_ATT = B_MIX + 12
        B_FFN2 = B_ATT + 16

        tiles_e3 = [(0, 352), (352, 352), (704, 352)]
        tiles_o2 = [(0, 512), (512, 512)]
        ffn_phase("f1", C_FFN1, B_FFN1, 0, TE, tiles_e3)

        def mixer_phase():
            ar = Arena()
            rows = ar.f32(3072)
            hbq = ar.bf16(KD * 288)
            aext = ar.bf16(8 * 288)
            dg = [ar.bf16(CW * 128) for _ in range(2)]
            cv2 = [ar.f32(8 * TQ) for _ in range(2)]
            gu2 = [ar.f32(8 * TQ) for _ in range(2)]
            vtmp = [ar.f32(1024) for _ in range(2)]
            vn = [ar.bf16(1024) for _ in range(2)]
            yoff = ar.off
            yb = ar.bf16(KD * TQ)
            ws32 = arena[:, yoff:yoff + 1024]
            stt = [ar.f32(TQ) for _ in range(3)]
            sq = [ar.bf16(288) for _ in range(2)]
            sg = [ar.f32(288) for _ in range(2)]
            rstd = ar.f32(288)
            rstd_n = ar.f32(288)
            hbk = [(hbq[:, k * 288:(k + 1) * 288], ("mhb", k)) for k in range(KD)]
            ak = [(aext[:, c * 288:(c + 1) * 288], ("ma", c)) for c in range(8)]
            cvk2 = [[(cv2[p][:, c * TQ:(c + 1) * TQ], ("mcv", p, c)) for c in range(8)] for p in range(2)]
            guk2 = [[(gu2[p][:, c * TQ:(c + 1) * TQ], ("mgu", p, c)) for c in range(8)] for p in range(2)]
            yk = [(yb[:, c * TQ:(c + 1) * TQ], ("my", c)) for c in range(KD)]
            a3 = aext.rearrange("p (c t) -> p c t", c=8)

            def setup(ws, wk):
                S.barrier()
                S.add("sp", lambda e: e.dma_start(out=rows, in_=rows_d.partition_broadcast(128)),
                      writes=[("rows",)], dma_sem="rin")
                S.add("sp", lambda e: e.dma_start(out=ws32, in_=wsT_d), writes=[("my", c) for c in range(8)], dma_sem="win")
                S.add("dve", lambda e: e.tensor_copy(wmT[:].rearrange("p h i -> p (h i)"), ws32),
                      reads=[("my", c) for c in range(8)], writes=[("wmT",)])
                S.add("dve", lambda e: e.memset(wmT[64:128, :, 0:64], 0.0), reads=[("wmT",)], writes=[("wmT",)])
                idi = stt[0].bitcast(I32)[:, 0:128]
                idf = stt[1][:, 0:128]
                S.add("pool", lambda e: e.iota(idi, pattern=[[1, 128]], base=0, channel_multiplier=-1),
                      writes=[("mmean",)])
                S.add("dve", lambda e: e.tensor_copy(idf, idi), reads=[("mmean",)], writes=[("mvar",)])
                S.add("dve", lambda e: e.tensor_single_scalar(idf, idf, 0.0, op=ALU.is_equal),
                      reads=[("mvar",)], writes=[("mvar",)])
                S.add("dve", lambda e: e.tensor_copy(ident_bf[:], idf), reads=[("mvar",)], writes=[("ident",)])
                rms_stats(0)
                rms_apply(0)
            items.append((None, setup))

            def tp(tq):
                e0 = 0 if tq == 0 else HALO + TQ * tq
                ne = 288 if tq == 0 else TQ
                po = HALO if tq == 0 else 0
                aoff = 0 if tq == 0 else HALO
                oe0 = HALO + TQ * tq
                return e0, ne, po, aoff, oe0

            def rms_stats(tq):
                e0, ne, po, aoff, oe0 = tp(tq)
                b = banks(1)[0]
                for k in range(KD):
                    sl = k % 2
                    S.add("act", lambda e, k=k, sl=sl: e.activation(sq[sl][:, 0:ne], x32[:, k, e0:e0 + ne], AF.Square),
                          reads=[("x", k)], writes=[("sqbuf", sl)])
                    S.add("pe", lambda e, k=k, sl=sl: e.matmul(psum[:, b, 0:ne], ones_bf[:], sq[sl][:, 0:ne],
                                                               start=(k == 0), stop=(k == KD - 1)),
                          reads=[("sqbuf", sl), ("ones",)], writes=[pkey(b)])
                S.add("act", lambda e: e.activation(rstd_n[:, 0:ne], psum[:, b, 0:ne], AF.Sqrt, bias=eps_rms, scale=1.0 / D),
                      reads=[pkey(b), ("c0",)], writes=[("rstdn",)])
                S.add("dve", lambda e: e.reciprocal(rstd_n[:, 0:ne], rstd_n[:, 0:ne]), reads=[("rstdn",)], writes=[("rstdn",)])

            def rms_apply(tq):
                e0, ne, po, aoff, oe0 = tp(tq)
                for k in range(KD):
                    S.add("dve", lambda e, k=k: e.scalar_tensor_tensor(
                        out=hbk[k][0][:, 0:ne], in0=x32[:, k, e0:e0 + ne], scalar=cols[:, C_MIX + k:C_MIX + k + 1],
                        in1=rstd_n[:, 0:ne], op0=ALU.mult, op1=ALU.mult),
                        reads=[("x", k), ("rstdn",), ("cols",)], writes=[hbk[k][1]])

            def emit_dg(c):
                sl = c % 2
                wc = C_CONVW + c * CW
                dgv = dg[sl].rearrange("p (j m) -> p j m", j=CW)
                S.add("dve", lambda e: e.tensor_tensor(
                    out=dgv, in0=ident_bf[:].unsqueeze(1).to_broadcast([128, CW, 128]),
                    in1=cols[:, wc:wc + CW].unsqueeze(2).to_broadcast([128, CW, 128]), op=ALU.mult),
                    reads=[("ident",), ("cols",)], writes=[("dg", sl)])

            def emit_conv(tq, c):
                sl = c % 2
                cvk = cvk2[tq % 2]
                dgv = dg[sl].rearrange("p (j m) -> p j m", j=CW)
                b = banks(1)[0]

                def mm(e):
                    ins = None
                    for j in range(CW):
                        ins = e.matmul(psum[:, b, 0:TQ], dgv[:, j, :], ak[c][0][:, 2 + j:2 + j + TQ],
                                       start=(j == 0), stop=(j == CW - 1))
                    return ins
                S.add("pe", mm, reads=[("dg", sl), ak[c][1]], writes=[pkey(b)])
                S.add("act", lambda e: e.activation(cvk[c][0], psum[:, b, 0:TQ], AF.Identity,
                                                    bias=cols[:, C_CONVB + c:C_CONVB + c + 1], scale=1.0),
                      reads=[pkey(b), ("cols",)], writes=[cvk[c][1]])

            sgc = [0]

            def conv_block_item(tq, i, after=None):
                e0, ne, po, aoff, oe0 = tp(tq)
                tl = [(0, ne)]

                def wconv(ws, wk):
                    if i == 0 and tq > 0:
                        S.add("act", lambda e: e.activation(a3[:, :, 0:HALO], a3[:, :, TQ:TQ + HALO], AF.Copy),
                              reads=[k for _, k in ak], writes=[k for _, k in ak])
                    if i >= 1:
                        emit_dg(2 * i - 2)
                        emit_dg(2 * i - 1)
                    for jj in range(2):
                        c = 2 * i + jj
                        bv = banks(1)
                        bg = banks(1)
                        mm_group(ws, wk, KD, 512, jj * 256, hbk, tl, bv)
                        mm_group(ws, wk, KD, 512, jj * 256 + 128, hbk, tl, bg)
                        sl = sgc[0] % 2
                        sgc[0] += 1
                        S.add("act", lambda e, bg=bg, sl=sl: e.activation(sg[sl][:, 0:ne], psum[:, bg[0], 0:ne], AF.Sigmoid),
                              reads=[pkey(bg[0])], writes=[("msg", sl)])
                        S.add("dve", lambda e, bv=bv, sl=sl, c=c: e.tensor_tensor(
                            out=ak[c][0][:, aoff:aoff + ne], in0=sg[sl][:, 0:ne], in1=psum[:, bv[0], 0:ne], op=ALU.mult),
                            reads=[("msg", sl), pkey(bv[0])], writes=[ak[c][1]])
                    if i >= 1:
                        emit_conv(tq, 2 * i - 2)
                        emit_conv(tq, 2 * i - 1)
                    if after is not None:
                        after()
                items.append((B_MIX + i, wconv))

            def u_block_item(tq, ub, after=None):
                e0, ne, po, aoff, oe0 = tp(tq)
                tl = [(0, ne)]
                guk = guk2[tq % 2]

                def wu(ws, wk):
                    if ub == 0:
                        emit_dg(6)
                        emit_dg(7)
                    for mmi in range(4):
                        c = ub * 4 + mmi
                        bu = banks(1)
                        mm_group(ws, wk, KD, 512, mmi * 128, hbk, tl, bu)
                        S.add("act", lambda e, bu=bu, c=c: e.activation(guk[c][0], psum[:, bu[0], po:po + TQ], AF.Gelu),
                              reads=[pkey(bu[0])], writes=[guk[c][1]])
                    if ub == 0:
                        emit_conv(tq, 6)
                        emit_conv(tq, 7)
                    if after is not None:
                        after()
                items.append((B_MIX + 4 + ub, wu))

            def v_block_item(tq, vb, after=None):
                e0, ne, po, aoff, oe0 = tp(tq)

                def wvb(ws, wk):
                    for s_ in range(2):
                        b = banks(1)[0]

                        def mm(e, s_=s_, b=b):
                            ins = None
                            for k in range(KD):
                                ins = e.matmul(psum[:, b, 0:512], hbk[k][0][:, po + s_ * 128:po + (s_ + 1) * 128],
                                               ws[:, k * 512:(k + 1) * 512], start=(k == 0), stop=(k == KD - 1))
                            return ins
                        S.add("pe", mm, reads=[wk] + [k for _, k in hbk], writes=[pkey(b)])
                        S.add("act", lambda e, s_=s_, b=b: e.activation(
                            vtmp[s_][:, vb * 512:(vb + 1) * 512], psum[:, b, 0:512], AF.Gelu,
                            accum_out=stat[:, s_ * 2 + vb:s_ * 2 + vb + 1]),
                            reads=[pkey(b)], writes=[("vt", s_, vb), ("vsum", s_, vb)])
                    if after is not None:
                        after()
                items.append((B_MIX + 6 + vb, wvb))

            def mix_out_item(tq, cb):
                e0, ne, po, aoff, oe0 = tp(tq)

                def wmo(ws, wk):
                    for mmi in range(4):
                        m = cb * 4 + mmi
                        b = banks(1)
                        mm_group(ws, wk, KD, 512, mmi * 128, yk, [(0, TQ)], b)
                        xa = x32[:, m, oe0:oe0 + TQ]
                        S.add("dve", lambda e, b=b, xa=xa: e.tensor_tensor(out=xa, in0=xa, in1=psum[:, b[0], 0:TQ], op=ALU.add),
                              reads=[pkey(b[0]), ("x", m)], writes=[("x", m)])
                items.append((B_MIX + 8 + cb, wmo))

            def chain_segments(tq):
                cvk = cvk2[tq % 2]
                guk = guk2[tq % 2]
                mean, var, tmp = stt[0], stt[1], stt[2]

                def seg_vln():
                    for s_ in range(2):
                        vt = vtmp[s_]
                        vk = [("vt", s_, 0), ("vt", s_, 1)]
                        c_sum, c_sq, c_mean, c_var, c_nm = 8 + s_ * 8, 9 + s_ * 8, 10 + s_ * 8, 11 + s_ * 8, 12 + s_ * 8
                        S.add("act", lambda e, vt=vt, c_sq=c_sq, s_=s_: e.activation(vn[s_], vt, AF.Square, accum_out=stat[:, c_sq:c_sq + 1]),
                              reads=vk, writes=[("vn", s_), ("vst", s_, 0)])
                        S.add("dve", lambda e, s_=s_, c_sum=c_sum: e.tensor_tensor(
                            out=stat[:, c_sum:c_sum + 1], in0=stat[:, s_ * 2:s_ * 2 + 1], in1=stat[:, s_ * 2 + 1:s_ * 2 + 2], op=ALU.add),
                            reads=[("vsum", s_, 0), ("vsum", s_, 1)], writes=[("vst", s_, 1)])
                        S.add("dve", lambda e, c_sum=c_sum, c_mean=c_mean: e.tensor_scalar(
                            stat[:, c_mean:c_mean + 1], stat[:, c_sum:c_sum + 1], 1.0 / 1024, None, op0=ALU.mult),
                            reads=[("vst", s_, 1)], writes=[("vst", s_, 2)])
                        S.add("dve", lambda e, c_mean=c_mean, c_nm=c_nm: e.tensor_tensor(
                            out=stat[:, c_nm:c_nm + 1], in0=stat[:, c_mean:c_mean + 1], in1=stat[:, c_mean:c_mean + 1], op=ALU.mult),
                            reads=[("vst", s_, 2)], writes=[("vst", s_, 3)])
                        S.add("dve", lambda e, c_sq=c_sq, c_nm=c_nm, c_var=c_var: e.scalar_tensor_tensor(
                            out=stat[:, c_var:c_var + 1], in0=stat[:, c_sq:c_sq + 1], scalar=1.0 / 1024,
                            in1=stat[:, c_nm:c_nm + 1], op0=ALU.mult, op1=ALU.subtract),
                            reads=[("vst", s_, 0), ("vst", s_, 3)], writes=[("vst", s_, 4)])
                        S.add("act", lambda e, c_var=c_var: e.activation(stat[:, c_var:c_var + 1], stat[:, c_var:c_var + 1],
                                                                         AF.Sqrt, bias=eps_ln, scale=1.0),
                              reads=[("vst", s_, 4), ("c1",)], writes=[("vst", s_, 4)])
                        S.add("dve", lambda e, c_var=c_var: e.reciprocal(stat[:, c_var:c_var + 1], stat[:, c_var:c_var + 1]),
                              reads=[("vst", s_, 4)], writes=[("vst", s_, 4)])
                        S.add("dve", lambda e, vt=vt, c_mean=c_mean, c_var=c_var: e.tensor_scalar(
                            vt, vt, stat[:, c_mean:c_mean + 1], stat[:, c_var:c_var + 1], op0=ALU.subtract, op1=ALU.mult),
                            reads=vk + [("vst", s_, 2), ("vst", s_, 4)], writes=vk)
                        S.add("dve", lambda e, vt=vt: e.tensor_tensor(out=vt, in0=vt, in1=rows[:, 0:1024], op=ALU.mult),
                              reads=vk + [("rows",)], writes=vk)
                        S.add("dve", lambda e, vt=vt, s_=s_: e.tensor_tensor(out=vn[s_], in0=vt, in1=rows[:, 1024:2048], op=ALU.add),
                              reads=vk + [("rows",)], writes=[("vn", s_)])

                def seg_cstats():
                    b1 = banks(1)[0]
                    b2 = banks(1)[0]
                    for c in range(8):
                        sl = c % 2
                        S.add("act", lambda e, c=c, sl=sl: e.activation(sq[sl][:, 0:TQ], cvk[c][0], AF.Copy),
                              reads=[cvk[c][1]], writes=[("sqbuf", sl)])
                        S.add("pe", lambda e, c=c, sl=sl: e.matmul(psum[:, b1, 0:TQ], ones_bf[:], sq[sl][:, 0:TQ],
                                                                   start=(c == 0), stop=(c == 7)),
                              reads=[("sqbuf", sl), ("ones",)], writes=[pkey(b1)])
                    for c in range(8):
                        sl = c % 2
                        S.add("act", lambda e, c=c, sl=sl: e.activation(sq[sl][:, 0:TQ], cvk[c][0], AF.Square),
                              reads=[cvk[c][1]], writes=[("sqbuf", sl)])
                        S.add("pe", lambda e, c=c, sl=sl: e.matmul(psum[:, b2, 0:TQ], ones_bf[:], sq[sl][:, 0:TQ],
                                                                   start=(c == 0), stop=(c == 7)),
                              reads=[("sqbuf", sl), ("ones",)], writes=[pkey(b2)])
                    S.add("dve", lambda e: e.tensor_scalar(mean, psum[:, b1, 0:TQ], 1.0 / 1024, None, op0=ALU.mult),
                          reads=[pkey(b1)], writes=[("mmean",)])
                    S.add("dve", lambda e: e.tensor_tensor(out=var, in0=mean, in1=mean, op=ALU.mult),
                          reads=[("mmean",)], writes=[("mvar",)])
                    S.add("dve", lambda e: e.scalar_tensor_tensor(out=var, in0=psum[:, b2, 0:TQ], scalar=1.0 / 1024, in1=var,
                                                                  op0=ALU.mult, op1=ALU.subtract),
                          reads=[pkey(b2), ("mvar",)], writes=[("mvar",)])
                    S.add("act", lambda e: e.activation(var, var, AF.Sqrt, bias=eps_ln, scale=1.0),
                          reads=[("mvar",), ("c1",)], writes=[("mvar",)])
                    S.add("dve", lambda e: e.reciprocal(var, var), reads=[("mvar",)], writes=[("mvar",)])

                def seg_capply():
                    for c in range(8):
                        cvc, ck = cvk[c]
                        S.add("dve", lambda e, cvc=cvc: e.tensor_tensor(out=cvc, in0=cvc, in1=mean, op=ALU.subtract),
                              reads=[ck, ("mmean",)], writes=[ck])
                        S.add("dve", lambda e, cvc=cvc, c=c: e.scalar_tensor_tensor(
                            out=cvc, in0=cvc, scalar=cols[:, C_CLNG + c:C_CLNG + c + 1], in1=var, op0=ALU.mult, op1=ALU.mult),
                            reads=[ck, ("mvar",), ("cols",)], writes=[ck])
                        S.add("act", lambda e, cvc=cvc, c=c: e.activation(cvc, cvc, AF.Silu,
                                                                          bias=cols[:, C_CLNB + c:C_CLNB + c + 1], scale=1.0),
                              reads=[ck, ("cols",)], writes=[ck])

                def seg_rmsa():
                    rms_feat("mra", cvk, C_ONC, [(yk[c][0], yk[c][1]) for c in range(8)], TQ, [(0, TQ)], 1024, sq, rstd)

                def seg_sgu():
                    for h in range(8):
                        b = banks(1)[0]

                        def sgm(e, h=h, b=b):
                            ins = None
                            for s_ in range(2):
                                ins = e.matmul(psum[:, b, s_ * 128:(s_ + 1) * 128], vn[s_][:, h * 128:(h + 1) * 128],
                                               wmT[:, h, :], start=True, stop=True)
                            return ins
                        S.add("pe", sgm, reads=[("vn", 0), ("vn", 1), ("wmT",)], writes=[pkey(b)])
                        bsb = rows[:, 2048 + h * 128:2048 + (h + 1) * 128]
                        S.add("dve", lambda e, b=b, bsb=bsb: e.tensor_tensor(
                            out=tmp.rearrange("p (s i) -> p s i", s=2), in0=psum[:, b, 0:TQ].rearrange("p (s i) -> p s i", s=2),
                            in1=bsb.unsqueeze(1).to_broadcast([128, 2, 128]), op=ALU.add),
                            reads=[pkey(b), ("rows",)], writes=[("mtmp", 0)])
                        S.add("dve", lambda e, h=h: e.tensor_tensor(out=guk[h][0], in0=guk[h][0], in1=tmp, op=ALU.mult),
                              reads=[guk[h][1], ("mtmp", 0)], writes=[guk[h][1]])

                def seg_rmsb():
                    rms_feat("mrb", guk, C_ONS, [(yk[8 + c][0], yk[8 + c][1]) for c in range(8)], TQ, [(0, TQ)], 1024, sq, rstd)

                return [seg_cstats, seg_capply, seg_rmsa, seg_vln, seg_sgu, seg_rmsb]

            for i in range(4):
                conv_block_item(0, i)
            for ub in range(2):
                u_block_item(0, ub)
            v_block_item(0, 0, after=(lambda: rms_stats(1)))
            v_block_item(0, 1)
            for tq in range(NT_MIX):
                segs = chain_segments(tq)
                if tq + 1 < NT_MIX:
                    items.append((None, lambda ws, wk, tq=tq: rms_apply(tq + 1)))
                    for i in range(4):
                        conv_block_item(tq + 1, i, after=segs[i])
                    for ub in range(2):
                        u_block_item(tq + 1, ub, after=segs[4 + ub])
                else:
                    items.append((None, lambda ws, wk, segs=segs: [s_() for s_ in segs]))
                for cb in range(4):
                    mix_out_item(tq, cb)
                if tq + 1 < NT_MIX:
                    nxt2 = (lambda tq=tq: rms_stats(tq + 2)) if tq + 2 < NT_MIX else None
                    v_block_item(tq + 1, 0, after=nxt2)
                    v_block_item(tq + 1, 1)

        if stop_after != "f1":
            mixer_phase()

        def attn_phase():
            ar = Arena()
            qb_ = ar.bf16(KD * TOK)
            hb = ar.bf16(KD * TOK)
            kT = ar.bf16(KD * NMEM)
            vv = ar.bf16(2 * D)
            pT = ar.bf16(2 * TOK)
            e32 = [ar.f32(NMEM) for _ in range(2)]
            pb = [ar.bf16(NMEM) for _ in range(2)]
            sq = [ar.bf16(TOK) for _ in range(2)]
            rstd = ar.f32(TOK)
            rstd_m = ar.f32(NMEM)
            m32 = arena[:, 0:KD * NMEM]
            mn = arena[:, KD * NMEM:KD * NMEM + KD * NMEM // 2].bitcast(BF16)
            qk = [(qb_[:, c * TOK:(c + 1) * TOK], ("aq", c)) for c in range(KD)]
            hbk = [(hb[:, k * TOK:(k + 1) * TOK], ("ahb", k)) for k in range(KD)]
            kTk = [(kT[:, c * NMEM:(c + 1) * NMEM], ("akT", c)) for c in range(KD)]
            m32k = [(m32[:, k * NMEM:(k + 1) * NMEM], ("am32", k)) for k in range(KD)]
            mnk = [(mn[:, k * NMEM:(k + 1) * NMEM], ("amn", k)) for k in range(KD)]
            scale = 512.0 ** -0.5
            memT_v = memT_d.rearrange("(k p) m -> p k m", p=128)

            def pre(ws, wk):
                S.barrier()
                for k in range(KD):
                    S.add("sp", lambda e, k=k: e.dma_start(out=m32k[k][0], in_=memT_v[:, k, :]),
                          writes=[m32k[k][1]], dma_sem=f"min{k}")
                rms_split(m32k, C_MEM, mnk, NMEM, [(0, NMEM)], D, sq, rstd_m, rkey=("rstdm",))
            items.append((None, pre))
            cpc = [0]

            def evac(dst, src, reads, writes):
                eng = "act" if cpc[0] % 2 == 0 else "dve"
                cpc[0] += 1
                if eng == "act":
                    S.add("act", lambda e: e.activation(dst, src, AF.Copy), reads=reads, writes=writes)
                else:
                    S.add("dve", lambda e: e.tensor_copy(dst, src), reads=reads, writes=writes)

            for kb in range(4):
                def wkb(ws, wk, kb=kb):
                    for mmi in range(4):
                        c = kb * 4 + mmi
                        b = banks(1)
                        mm_group(ws, wk, KD, 512, mmi * 128, mnk, [(0, NMEM)], b)
                        S.add("dve", lambda e, c=c, b=b: e.tensor_tensor(out=kTk[c][0], in0=psum[:, b[0], 0:NMEM],
                                                                         in1=rstd_m[:, 0:NMEM], op=ALU.mult),
                              reads=[pkey(b[0]), ("rstdm",)], writes=[kTk[c][1]])
                items.append((B_ATT + kb, wkb))
            for vb in range(4):
                def wvb(ws, wk, vb=vb):
                    for mc in range(2):
                        b = banks(1)[0]

                        def mm(e, mc=mc, b=b):
                            ins = None
                            for k in range(KD):
                                ins = e.matmul(psum[:, b, 0:512], mnk[k][0][:, mc * 128:(mc + 1) * 128],
                                               ws[:, k * 512:(k + 1) * 512], start=(k == 0), stop=(k == KD - 1))
                            return ins
                        S.add("pe", mm, reads=[wk] + [k for _, k in mnk], writes=[pkey(b)])
                        evac(vv[:, mc * D + vb * 512:mc * D + (vb + 1) * 512], psum[:, b, 0:512], [pkey(b)], [("av", mc, vb)])
                items.append((B_ATT + 4 + vb, wvb))

            def pre2(ws, wk):
                S.barrier()
                xs = [(x32[:, k, HALO:TE], ("x", k)) for k in range(KD)]
                rms_split(xs, C_XATT, hbk, TOK, tiles_o2, D, sq, rstd)
            items.append((None, pre2))
            for qb in range(4):
                def wqb(ws, wk, qb=qb):
                    for mmi in range(4):
                        c = qb * 4 + mmi
                        bs = banks(2)
                        mm_group(ws, wk, KD, 512, mmi * 128, hbk, tiles_o2, bs)
                        for ti, (t0, nn) in enumerate(tiles_o2):
                            S.add("dve", lambda e, c=c, b=bs[ti], t0=t0, nn=nn: e.tensor_tensor(
                                out=qk[c][0][:, t0:t0 + nn], in0=psum[:, b, 0:nn], in1=rstd[:, t0:t0 + nn], op=ALU.mult),
                                reads=[pkey(bs[ti]), ("rstdbuf",)], writes=[("aq", c, ti)])
                items.append((B_ATT + 8 + qb, wqb))

            def core(ws, wk):
                vkeys = [("av", mc, vb) for mc in range(2) for vb in range(4)]
                sbank = {}

                def emit_sc(it):
                    hh, tt = divmod(it, 8)
                    b = banks(1)[0]
                    sbank[it] = b
                    ti = tt // 4

                    def sc(e):
                        ins = None
                        for i in range(4):
                            c = 4 * hh + i
                            ins = e.matmul(psum[:, b, 0:NMEM], qk[c][0][:, tt * 128:(tt + 1) * 128], kTk[c][0],
                                           start=(i == 0), stop=(i == 3))
                        return ins
                    S.add("pe", sc, reads=[("aq", 4 * hh + i, ti) for i in range(4)] + [kTk[4 * hh + i][1] for i in range(4)],
                          writes=[pkey(b)])

                def emit_chain(it):
                    sl = it % 2
                    b = sbank[it]
                    cm = 16 + sl * 4
                    S.add("dve", lambda e: e.reduce_max(stat[:, cm:cm + 1], psum[:, b, 0:NMEM], axis=AX.X),
                          reads=[pkey(b)], writes=[("as", sl, 0)])
                    S.add("dve", lambda e: e.tensor_scalar(stat[:, cm + 1:cm + 2], stat[:, cm:cm + 1], -scale, None, op0=ALU.mult),
                          reads=[("as", sl, 0)], writes=[("as", sl, 1)])
                    S.add("act", lambda e: e.activation(
                        e32[sl], psum[:, b, 0:NMEM], AF.Exp, bias=stat[:, cm + 1:cm + 2], scale=scale,
                        accum_out=stat[:, cm + 2:cm + 3]),
                        reads=[pkey(b), ("as", sl, 1)], writes=[("ae", sl), ("as", sl, 2)])
                    S.add("dve", lambda e: e.reciprocal(stat[:, cm + 3:cm + 4], stat[:, cm + 2:cm + 3]),
                          reads=[("as", sl, 2)], writes=[("as", sl, 3)])
                    S.add("dve", lambda e: e.scalar_tensor_tensor(out=pb[sl], in0=e32[sl], scalar=stat[:, cm + 3:cm + 4],
                                                                  in1=rstd_m[:, 0:NMEM], op0=ALU.mult, op1=ALU.mult),
                          reads=[("ae", sl), ("as", sl, 3), ("rstdm",)], writes=[("ap", sl)])

                def emit_tr(it):
                    hh, tt = divmod(it, 8)
                    sl = it % 2
                    bt = banks(1)[0]
                    ptv = psum[:, bt, 0:128].bitcast(BF16)

                    def tr(e):
                        ins = None
                        for mc in range(2):
                            ins = e.transpose(ptv[:, mc * 128:(mc + 1) * 128], pb[sl][:, mc * 128:(mc + 1) * 128], ident_bf[:])
                        return ins
                    S.add("pe", tr, reads=[("ap", sl), ("ident",)], writes=[pkey(bt)])
                    dst = pT.rearrange("p (m t) -> p m t", m=2)[:, :, tt * 128:(tt + 1) * 128]
                    evac(dst, ptv.rearrange("p (m t) -> p m t", m=2), [pkey(bt)], [("apT", tt)])

                def emit_pv(hh):
                    for i in range(4):
                        c = 4 * hh + i
                        bs = banks(2)

                        def pv(e, c=c, bs=bs):
                            ins = None
                            for mc in range(2):
                                for ti, (t0, nn) in enumerate(tiles_o2):
                                    ins = e.matmul(psum[:, bs[ti], 0:nn], vv[:, mc * D + c * 128:mc * D + (c + 1) * 128],
                                                   pT[:, mc * TOK + t0:mc * TOK + t0 + nn], start=(mc == 0), stop=(mc == 1))
                            return ins
                        S.add("pe", pv, reads=vkeys + [("apT", tt) for tt in range(8)], writes=[pkey(b) for b in bs])
                        for ti, (t0, nn) in enumerate(tiles_o2):
                            evac(hbk[c][0][:, t0:t0 + nn], psum[:, bs[ti], 0:nn], [pkey(bs[ti])] + [hbk[c][1]], [hbk[c][1]])

                NIT = 32
                emit_sc(0)
                emit_chain(0)
                for it in range(1, NIT + 1):
                    if it < NIT:
                        emit_sc(it)
                        emit_chain(it)
                    emit_tr(it - 1)
                    if it % 8 == 0:
                        emit_pv(it // 8 - 1)
            items.append((None, core))
            for ob in range(4):
                def wob(ws, wk, ob=ob):
                    for mmi in range(4):
                        m = ob * 4 + mmi
                        bs = banks(2)
                        mm_group(ws, wk, KD, 512, mmi * 128, hbk, tiles_o2, bs)

                        def acc(e, bs=bs, m=m):
                            ins = None
                            for ti, (t0, nn) in enumerate(tiles_o2):
                                xa = x32[:, m, HALO + t0:HALO + t0 + nn]
                                ins = e.tensor_tensor(out=xa, in0=xa, in1=psum[:, bs[ti], 0:nn], op=ALU.add)
                            return ins
                        S.add("dve", acc, reads=[pkey(b) for b in bs] + [("x", m)], writes=[("x", m)])
                items.append((B_ATT + 12 + ob, wob))

        if stop_after not in ("f1", "mix"):
            attn_phase()
        if stop_after not in ("f1", "mix", "att"):
            ffn_phase("f2", C_FFN2, B_FFN2, HALO, TOK, tiles_o2)

        def final(ws, wk):
            S.barrier()
            ar = Arena()
            sq = [ar.bf16(TOK) for _ in range(2)]
            rstd = ar.f32(TOK)
            xs = [(x32[:, k, HALO:TE], ("x", k)) for k in range(KD)]
            if stop_after is None:
                rms_feat("fin", xs, C_FINAL, xs, TOK, tiles_o2, D, sq, rstd)
            for k in range(KD):
                S.add("sp", lambda e, k=k: e.dma_start(out=out_d[k * 128:(k + 1) * 128, :], in_=x32[:, k, HALO:TE]),
                      reads=[("x", k)], dma_sem=f"osem{k % 4}")
        items.append((None, final))

        wq = [(i, it[0]) for i, it in enumerate(items) if it[0] is not None]
        loaded = {}
        nxt = [0]

        def prefetch():
            if nxt[0] < len(wq):
                ii, bidx = wq[nxt[0]]
                nxt[0] += 1
                loaded[ii] = wload(int(offs[bidx]), int(lens[bidx]))

        for _ in range(NSLOT):
            prefetch()
        for i, (bidx, fn) in enumerate(items):
            if bidx is None:
                fn(None, None)
            else:
                ws, wk = loaded.pop(i)
                fn(ws, wk)
                prefetch()
        finals = [(f"osem{i}", S.count[f"osem{i}"]) for i in range(4)]
        S.emit(nc, final_waits=finals)
    return nc


def _prep_inputs(inp):
    x = np.asarray(inp["x"], np.float32)
    mem = np.asarray(inp["mem"], np.float32)
    inp = {k: np.asarray(v, np.float32) for k, v in inp.items()}
    wstream, _ = _build_wstream(inp)
    cols = _build_cols(inp)
    rows = np.concatenate([inp["sgu_ln_g"][0], inp["sgu_ln_b"][0], inp["sgu_b"][0].reshape(-1)])[None, :]
    rows = np.ascontiguousarray(rows, np.float32)
    wsT = np.ascontiguousarray(inp["sgu_w"][0].transpose(2, 0, 1).reshape(128, 1024))
    in_maps = []
    for c in range(8):
        b, q = divmod(c, 4)
        s0 = q * TOK
        if q == 0:
            xe = np.concatenate([np.zeros((HALO, D), np.float32), x[b, 0:TOK]], axis=0)
        else:
            xe = x[b, s0 - HALO:s0 + TOK]
        in_maps.append({
            "xT": np.ascontiguousarray(xe.T),
            "memT": np.ascontiguousarray(mem[b].T),
            "wstream": wstream,
            "cols": cols,
            "rows": rows,
            "wsT": wsT,
        })
    return in_maps


def kernel(**inputs):
    stop_after = os.environ.get("MK_STOP") or None
    in_maps = _prep_inputs(inputs)
    nc = build_program(stop_after)
    ncores = int(os.environ.get("MK_CORES", "8"))
    res = run_bass_kernel_spmd(nc, in_maps[:ncores], core_ids=list(range(ncores)))
    out = np.zeros((2, SEQ, D), np.float32)
    for c in range(ncores):
        b, q = divmod(c, 4)
        out[b, q * TOK:(q + 1) * TOK, :] = res.results[c]["outT"].T
    return out
```

```python
import contextlib
import os
import numpy as np
import concourse.bass as bass
import concourse.mybir as mybir
from concourse.bass_utils import run_bass_kernel_spmd

F32 = mybir.dt.float32
BF16 = mybir.dt.bfloat16
I32 = mybir.dt.int32
AF = mybir.ActivationFunctionType
ALU = mybir.AluOpType
AX = mybir.AxisListType

D = 2048
DFF = 5632
KD = D // 128
NF = DFF // 128
SEQ = 4096
TOK = 1024
HALO = 32
TE = TOK + HALO
NMEM = 256
CW = 31
RMS_EPS = 1e-6
LN_EPS = 1e-5
FF_GROUPS = [(0, 12), (12, 24), (24, 34), (34, 44)]
NT_MIX = 4
TQ = TOK // NT_MIX
SLOT_ELEMS = 8192
NSLOT = 2

C_FFN1, C_MIX, C_XATT, C_MEM, C_FFN2, C_FINAL = 0, 16, 32, 48, 64, 80
C_CONVB, C_CLNG, C_CLNB, C_ONC, C_ONS, C_CONVW = 96, 104, 112, 120, 128, 136
NCOLS = C_CONVW + 8 * CW


ENGINES = ("pe", "act", "dve", "pool", "sp")


class _Op:
    __slots__ = ("fn", "waits", "ticket", "inc")


class Sched:
    def __init__(self):
        self.ops = {e: [] for e in ENGINES}
        self.count = {}
        self.last_writer = {}
        self.readers = {}
        self.waited = {e: {} for e in ENGINES}
        self.barrier_t = []

    def barrier(self):
        self.barrier_t = [(e, self.count.get(e, 0)) for e in ("pe", "act", "dve")]

    def add(self, eng, fn, reads=(), writes=(), dma_sem=None, nobarrier=False):
        deps = {}

        def need(t):
            if t is not None and deps.get(t[0], 0) < t[1]:
                deps[t[0]] = t[1]

        for key in reads:
            need(self.last_writer.get(key))
        for key in writes:
            need(self.last_writer.get(key))
            for t in self.readers.get(key, ()):
                need(t)
        if not nobarrier:
            for t in self.barrier_t:
                need(t)
        if dma_sem is not None:
            semkey, inc = dma_sem, 16
        else:
            semkey, inc = eng, 1
        val = self.count.get(semkey, 0) + inc
        self.count[semkey] = val
        ticket = (semkey, val)
        waits = []
        w = self.waited[eng]
        for k, v in deps.items():
            if v <= 0 or w.get(k, 0) >= v:
                continue
            w[k] = v
            waits.append((k, v))
        op = _Op()
        op.fn, op.waits, op.ticket, op.inc = fn, waits, ticket, inc
        self.ops[eng].append(op)
        for key in writes:
            self.last_writer[key] = ticket
            self.readers[key] = []
        for key in reads:
            if key not in writes:
                self.readers.setdefault(key, []).append(ticket)
        return ticket

    def emit(self, nc, final_waits=()):
        with contextlib.ExitStack() as st:
            sems = {k: st.enter_context(nc.semaphore("s_" + str(k))) for k in self.count}
            block = st.enter_context(nc.Block())

            def run(engname):
                def body(eng):
                    for op in self.ops[engname]:
                        for k, v in op.waits:
                            eng.wait_ge(sems[k], v)
                        ins = op.fn(eng)
                        ins.then_inc(sems[op.ticket[0]], op.inc)
                    if engname == "sp":
                        for k, v in final_waits:
                            eng.wait_ge(sems[k], v)
                return body

            block.tensor(run("pe"))
            block.scalar(run("act"))
            block.vector(run("dve"))
            block.gpsimd(run("pool"))
            block.sync(run("sp"))


def _blk(W, rows, cols):
    kc, c = len(rows), len(cols)
    ridx = (np.asarray(rows)[:, None] * 128 + np.arange(128)[None, :]).reshape(-1)
    sub = W[ridx][:, cols]
    return np.ascontiguousarray(sub.reshape(kc, 128, c).transpose(1, 0, 2).reshape(128, kc * c))


def _ffn_blocks(w_in, w_out):
    out = []
    ar = np.arange(128)
    allk = list(range(KD))
    for (g0, g1) in FF_GROUPS:
        for j0 in range(g0, g1, 2):
            cols = np.concatenate([j0 * 128 + ar, DFF + j0 * 128 + ar,
                                   (j0 + 1) * 128 + ar, DFF + (j0 + 1) * 128 + ar])
            out.append(_blk(w_in, allk, cols))
        for cb in range(4):
            out.append(_blk(w_out, list(range(g0, g1)), cb * 512 + np.arange(512)))
    return out


def _build_wstream(inp):
    blocks = []
    ar = np.arange(128)
    a512 = np.arange(512)
    allk = list(range(KD))
    blocks += _ffn_blocks(inp["ffn1_w_in"][0], inp["ffn1_w_out"][0])
    wmi = inp["w_mix_in"][0]
    mix = []
    for i in range(4):
        c0, c1 = 2 * i, 2 * i + 1
        cols = np.concatenate([c0 * 128 + ar, 1024 + c0 * 128 + ar, c1 * 128 + ar, 1024 + c1 * 128 + ar])
        mix.append(_blk(wmi, allk, cols))
    for ub in range(2):
        mix.append(_blk(wmi, allk, 2048 + ub * 512 + a512))
    for vb in range(2):
        mix.append(_blk(wmi, allk, 3072 + vb * 512 + a512))
    wmo = inp["w_mix_out"][0]
    for cb in range(4):
        mix.append(_blk(wmo, allk, cb * 512 + a512))
    blocks += mix
    wkv = inp["w_kv"][0]
    for kb in range(4):
        blocks.append(_blk(wkv, allk, kb * 512 + a512))
    for vb in range(4):
        blocks.append(_blk(wkv, allk, 2048 + vb * 512 + a512))
    for qb in range(4):
        blocks.append(_blk(inp["w_q"][0], allk, qb * 512 + a512))
    for ob in range(4):
        blocks.append(_blk(inp["w_o"][0], allk, ob * 512 + a512))
    blocks += _ffn_blocks(inp["ffn2_w_in"][0], inp["ffn2_w_out"][0])
    offs = np.cumsum([0] + [b.shape[1] for b in blocks])
    return np.concatenate(blocks, axis=1), offs


def _wstream_layout():
    ffn = []
    for (g0, g1) in FF_GROUPS:
        ffn += [KD * 512] * ((g1 - g0) // 2)
        ffn += [(g1 - g0) * 512] * 4
    lens = ffn + [KD * 512] * 12 + [KD * 512] * 16 + ffn
    offs = np.cumsum([0] + lens)
    return lens, offs


def _col(v):
    return np.asarray(v, np.float32).reshape(-1, 128).T


def _build_cols(inp):
    cols = np.zeros((128, NCOLS), np.float32)
    cols[:, C_FFN1:C_FFN1 + 16] = _col(inp["ffn1_norm"][0])
    cols[:, C_MIX:C_MIX + 16] = _col(inp["mix_norm"][0])
    cols[:, C_XATT:C_XATT + 16] = _col(inp["xattn_norm"][0])
    cols[:, C_MEM:C_MEM + 16] = _col(inp["mem_norm"][0])
    cols[:, C_FFN2:C_FFN2 + 16] = _col(inp["ffn2_norm"][0])
    cols[:, C_FINAL:C_FINAL + 16] = _col(inp["final_norm"])
    cols[:, C_CONVB:C_CONVB + 8] = _col(inp["conv_b"][0])
    cols[:, C_CLNG:C_CLNG + 8] = _col(inp["conv_ln_g"][0])
    cols[:, C_CLNB:C_CLNB + 8] = _col(inp["conv_ln_b"][0])
    cols[:, C_ONC:C_ONC + 8] = _col(inp["out_norm_conv"][0])
    cols[:, C_ONS:C_ONS + 8] = _col(inp["out_norm_sgu"][0])
    cw = np.asarray(inp["conv_w"][0], np.float32)
    cols[:, C_CONVW:] = cw.T.reshape(8, 128, CW).transpose(1, 0, 2).reshape(128, 8 * CW)
    return cols


def build_program(stop_after=None):
    nc = bass.Bass("TRN2", target_bir_lowering=False)
    lens, offs = _wstream_layout()
    WTOT = int(offs[-1])
    xT_d = nc.dram_tensor("xT", [D, TE], F32, kind="ExternalInput").ap()
    memT_d = nc.dram_tensor("memT", [D, NMEM], F32, kind="ExternalInput").ap()
    w_d = nc.dram_tensor("wstream", [128, WTOT], F32, kind="ExternalInput").ap()
    cols_d = nc.dram_tensor("cols", [128, NCOLS], F32, kind="ExternalInput").ap()
    rows_d = nc.dram_tensor("rows", [1, 3072], F32, kind="ExternalInput").ap()
    wsT_d = nc.dram_tensor("wsT", [128, 8 * 128], F32, kind="ExternalInput").ap()
    out_d = nc.dram_tensor("outT", [D, TOK], F32, kind="ExternalOutput").ap()

    st = contextlib.ExitStack()
    with st:
        def SB(name, shape, dt):
            return st.enter_context(nc.sbuf_tensor(name, shape, dt))

        x32 = SB("x32", [128, KD, TE], F32)
        cols = SB("cols_sb", [128, NCOLS], F32)
        ones_bf = SB("ones_bf", [128, 128], BF16)
        ident_bf = SB("ident_bf", [128, 128], BF16)
        wmT = SB("wmT", [128, 8, 128], BF16)
        consts = SB("consts", [128, 4], F32)
        stat = SB("stat", [128, 64], F32)
        wslots = [SB(f"wslot{i}", [128, SLOT_ELEMS], BF16) for i in range(NSLOT)]
        ARENA_F32 = 26368
        arena = SB("arena", [128, ARENA_F32], F32)
        psum = st.enter_context(nc.psum_tensor("psum", [128, 8, 512], F32))

        S = Sched()
        eps_rms = consts[:, 0:1]
        eps_ln = consts[:, 1:2]

        class Arena:
            def __init__(self):
                self.off = 0

            def f32(self, n):
                a = arena[:, self.off:self.off + n]
                self.off += n
                assert self.off <= ARENA_F32, self.off
                return a

            def bf16(self, n):
                assert n % 2 == 0
                a = arena[:, self.off:self.off + n // 2].bitcast(BF16)
                self.off += n // 2
                assert self.off <= ARENA_F32, self.off
                return a

        bank_ctr = [0]

        def banks(n):
            r = [(bank_ctr[0] + i) % 8 for i in range(n)]
            bank_ctr[0] += n
            return r

        def pkey(b):
            return ("ps", b)

        wcount = [0]

        def wload(off, L):
            slot = wcount[0] % NSLOT
            wcount[0] += 1
            key = ("w", slot)
            S.add("pool", lambda e, slot=slot, off=off, L=L: e.dma_start(
                out=wslots[slot][:, 0:L], in_=w_d[:, off:off + L]),
                writes=[key], dma_sem=f"wsem{slot}", nobarrier=True)
            return wslots[slot], key

        items = []

        def rms_feat(tag, srcs, gcol0, dsts, T, tiles, nfeat, sq, rstd, eps=eps_rms):
            n = len(srcs)
            bs = banks(len(tiles))
            for k, (sap, skey) in enumerate(srcs):
                sl = k % 2
                S.add("act", lambda e, sap=sap, sl=sl: e.activation(sq[sl][:, 0:T], sap, AF.Square),
                      reads=[skey], writes=[("sqbuf", sl)])

                def mm(e, k=k, sl=sl):
                    ins = None
                    for ti, (t0, nn) in enumerate(tiles):
                        ins = e.matmul(psum[:, bs[ti], 0:nn], ones_bf[:], sq[sl][:, t0:t0 + nn],
                                       start=(k == 0), stop=(k == n - 1))
                    return ins
                S.add("pe", mm, reads=[("sqbuf", sl)], writes=[pkey(b) for b in bs])

            def sq_(e):
                ins = None
                for ti, (t0, nn) in enumerate(tiles):
                    ins = e.activation(rstd[:, t0:t0 + nn], psum[:, bs[ti], 0:nn], AF.Sqrt,
                                       bias=eps, scale=1.0 / nfeat)
                return ins
            S.add("act", sq_, reads=[pkey(b) for b in bs] + [("c0",), ("c1",)], writes=[("rstdbuf",)])
            S.add("dve", lambda e: e.reciprocal(rstd[:, 0:T], rstd[:, 0:T]),
                  reads=[("rstdbuf",)], writes=[("rstdbuf",)])
            for k, ((sap, skey), (dap, dkey)) in enumerate(zip(srcs, dsts)):
                S.add("dve", lambda e, sap=sap, dap=dap, k=k: e.scalar_tensor_tensor(
                    out=dap, in0=sap, scalar=cols[:, gcol0 + k:gcol0 + k + 1], in1=rstd[:, 0:T],
                    op0=ALU.mult, op1=ALU.mult),
                    reads=[skey, ("rstdbuf",), ("cols",)] + ([dkey] if dkey != skey else []), writes=[dkey])

        def rms_split(srcs, gcol0, dsts, T, tiles, nfeat, sq, rstd, eps=eps_rms, rkey=("rstdbuf",)):
            n = len(srcs)
            bs = banks(len(tiles))
            for k, ((sap, skey), (dap, dkey)) in enumerate(zip(srcs, dsts)):
                sl = k % 2
                S.add("act", lambda e, sap=sap, dap=dap, k=k: e.activation(dap, sap, AF.Identity,
                                                                          scale=cols[:, gcol0 + k:gcol0 + k + 1]),
                      reads=[skey, ("cols",)], writes=[dkey])
                S.add("dve", lambda e, sap=sap, sl=sl: e.tensor_tensor(out=sq[sl][:, 0:T], in0=sap, in1=sap, op=ALU.mult),
                      reads=[skey], writes=[("sqbuf", sl)])

                def mm(e, k=k, sl=sl):
                    ins = None
                    for ti, (t0, nn) in enumerate(tiles):
                        ins = e.matmul(psum[:, bs[ti], 0:nn], ones_bf[:], sq[sl][:, t0:t0 + nn],
                                       start=(k == 0), stop=(k == n - 1))
                    return ins
                S.add("pe", mm, reads=[("sqbuf", sl), ("ones",)], writes=[pkey(b) for b in bs])

            def sq_(e):
                ins = None
                for ti, (t0, nn) in enumerate(tiles):
                    ins = e.activation(rstd[:, t0:t0 + nn], psum[:, bs[ti], 0:nn], AF.Sqrt, bias=eps, scale=1.0 / nfeat)
                return ins
            S.add("act", sq_, reads=[pkey(b) for b in bs] + [("c0",), ("c1",)], writes=[rkey])
            S.add("dve", lambda e: e.reciprocal(rstd[:, 0:T], rstd[:, 0:T]), reads=[rkey], writes=[rkey])

        def mm_group(wv, wkey, kc, C, co, rhs, tiles, bs):
            def mm(e):
                ins = None
                for k in range(kc):
                    lw = wv[:, k * C + co:k * C + co + 128]
                    for ti, (t0, nn) in enumerate(tiles):
                        ins = e.matmul(psum[:, bs[ti], 0:nn], lw, rhs[k][0][:, t0:t0 + nn],
                                       start=(k == 0), stop=(k == kc - 1))
                return ins
            S.add("pe", mm, reads=[wkey] + [r[1] for r in rhs[:kc]], writes=[pkey(b) for b in bs])

        xT_v = xT_d.rearrange("(k p) t -> p k t", p=128)
        for k in range(KD):
            S.add("sp", lambda e, k=k: e.dma_start(out=x32[:, k, :], in_=xT_v[:, k, :]),
                  writes=[("x", k)], dma_sem=f"xin{k}")
        S.add("sp", lambda e: e.dma_start(out=cols[:], in_=cols_d), writes=[("cols",)], dma_sem="cin")
        S.add("dve", lambda e: e.memset(ones_bf[:], 1.0), writes=[("ones",)])
        S.add("dve", lambda e: e.memset(consts[:, 0:1], RMS_EPS), writes=[("c0",)])
        S.add("dve", lambda e: e.memset(consts[:, 1:2], LN_EPS), writes=[("c1",)])
        XK = [("x", k) for k in range(KD)]

        def ffn_phase(tag, gcol0, blk0, t_lo, T, tiles):
            ar = Arena()
            hb = ar.bf16(KD * T)
            gb = ar.bf16(12 * T)
            sq = [ar.bf16(T + (T % 2)) for _ in range(2)]
            sg = [ar.f32(T) for _ in range(2)]
            un = [ar.f32(T) for _ in range(2)]
            rstd = ar.f32(T)
            hbk = [(hb[:, k * T:(k + 1) * T], (tag + "hb", k)) for k in range(KD)]
            gbk = [(gb[:, k * T:(k + 1) * T], (tag + "gb", k)) for k in range(12)]
            xs = [(x32[:, k, t_lo:t_lo + T], ("x", k)) for k in range(KD)]

            def pre(ws, wk):
                S.barrier()
                rms_split(xs, gcol0, hbk, T, tiles, D, sq, rstd)
            items.append((None, pre))
            bi = blk0
            sgc = [0]
            for (g0, g1) in FF_GROUPS:
                for j0 in range(g0, g1, 2):
                    def win(ws, wk, j0=j0, g0=g0):
                        for jj in range(2):
                            j = j0 + jj
                            bg = banks(len(tiles))
                            bu = banks(len(tiles))
                            mm_group(ws, wk, KD, 512, jj * 256, hbk, tiles, bg)
                            mm_group(ws, wk, KD, 512, jj * 256 + 128, hbk, tiles, bu)
                            sl = sgc[0] % 2
                            sgc[0] += 1

                            def gn(e, bg=bg, sl=sl):
                                ins = None
                                for ti, (t0, nn) in enumerate(tiles):
                                    ins = e.tensor_tensor(out=sg[sl][:, t0:t0 + nn], in0=psum[:, bg[ti], 0:nn],
                                                          in1=rstd[:, t0:t0 + nn], op=ALU.mult)
                                return ins
                            S.add("dve", gn, reads=[pkey(b) for b in bg] + [("rstdbuf",)], writes=[(tag + "sg", sl)])
                            S.add("act", lambda e, sl=sl: e.activation(sg[sl][:, 0:T], sg[sl][:, 0:T], AF.Silu),
                                  reads=[(tag + "sg", sl)], writes=[(tag + "sg", sl)])

                            def upn(e, bu=bu, sl=sl):
                                ins = None
                                for ti, (t0, nn) in enumerate(tiles):
                                    ins = e.tensor_tensor(out=un[sl][:, t0:t0 + nn], in0=psum[:, bu[ti], 0:nn],
                                                          in1=rstd[:, t0:t0 + nn], op=ALU.mult)
                                return ins
                            S.add("dve", upn, reads=[pkey(b) for b in bu] + [("rstdbuf",)], writes=[(tag + "un", sl)])
                            S.add("dve", lambda e, sl=sl, j=j, g0=g0: e.tensor_tensor(
                                out=gbk[j - g0][0], in0=sg[sl][:, 0:T], in1=un[sl][:, 0:T], op=ALU.mult),
                                reads=[(tag + "sg", sl), (tag + "un", sl)], writes=[gbk[j - g0][1]])
                    items.append((bi, win))
                    bi += 1
                kc = g1 - g0
                for cb in range(4):
                    def wout(ws, wk, cb=cb, kc=kc):
                        for mmi in range(4):
                            m = cb * 4 + mmi
                            bo = banks(len(tiles))
                            mm_group(ws, wk, kc, 512, mmi * 128, gbk, tiles, bo)

                            def acc(e, bo=bo, m=m):
                                ins = None
                                for ti, (t0, nn) in enumerate(tiles):
                                    xa = x32[:, m, t_lo + t0:t_lo + t0 + nn]
                                    ins = e.scalar_tensor_tensor(out=xa, in0=psum[:, bo[ti], 0:nn], scalar=0.5,
                                                                 in1=xa, op0=ALU.mult, op1=ALU.add)
                                return ins
                            S.add("dve", acc, reads=[pkey(b) for b in bo] + [("x", m)], writes=[("x", m)])
                    items.append((bi, wout))
                    bi += 1
            return bi

        nblk_ffn = sum((g1 - g0) // 2 + 4 for g0, g1 in FF_GROUPS)
        B_FFN1 = 0
        B_MIX = nblk_ffn
        B_ATT = B_MIX + 12
        B_FFN2 = B_ATT + 16

        tiles_e3 = [(0, 352), (352, 352), (704, 352)]
        tiles_o2 = [(0, 512), (512, 512)]
        ffn_phase("f1", C_FFN1, B_FFN1, 0, TE, tiles_e3)

        def mixer_phase():
            ar = Arena()
            rows = ar.f32(3072)
            hbq = ar.bf16(KD * 288)
            aext = ar.bf16(8 * 288)
            dg = [ar.bf16(CW * 128) for _ in range(2)]
            cv2 = [ar.f32(8 * TQ) for _ in range(2)]
            gu2 = [ar.f32(8 * TQ) for _ in range(2)]
            vtmp = [ar.f32(1024) for _ in range(2)]
            vn = [ar.bf16(1024) for _ in range(2)]
            yoff = ar.off
            yb = ar.bf16(KD * TQ)
            ws32 = arena[:, yoff:yoff + 1024]
            stt = [ar.f32(TQ) for _ in range(3)]
            sq = [ar.bf16(288) for _ in range(2)]
            sg = [ar.f32(288) for _ in range(2)]
            rstd = ar.f32(288)
            rstd_n = ar.f32(288)
            hbk = [(hbq[:, k * 288:(k + 1) * 288], ("mhb", k)) for k in range(KD)]
            ak = [(aext[:, c * 288:(c + 1) * 288], ("ma", c)) for c in range(8)]
            cvk2 = [[(cv2[p][:, c * TQ:(c + 1) * TQ], ("mcv", p, c)) for c in range(8)] for p in range(2)]
            guk2 = [[(gu2[p][:, c * TQ:(c + 1) * TQ], ("mgu", p, c)) for c in range(8)] for p in range(2)]
            yk = [(yb[:, c * TQ:(c + 1) * TQ], ("my", c)) for c in range(KD)]
            a3 = aext.rearrange("p (c t) -> p c t", c=8)

            def setup(ws, wk):
                S.barrier()
                S.add("sp", lambda e: e.dma_start(out=rows, in_=rows_d.partition_broadcast(128)),
                      writes=[("rows",)], dma_sem="rin")
                S.add("sp", lambda e: e.dma_start(out=ws32, in_=wsT_d), writes=[("my", c) for c in range(8)], dma_sem="win")
                S.add("dve", lambda e: e.tensor_copy(wmT[:].rearrange("p h i -> p (h i)"), ws32),
                      reads=[("my", c) for c in range(8)], writes=[("wmT",)])
                S.add("dve", lambda e: e.memset(wmT[64:128, :, 0:64], 0.0), reads=[("wmT",)], writes=[("wmT",)])
                idi = stt[0].bitcast(I32)[:, 0:128]
                idf = stt[1][:, 0:128]
                S.add("pool", lambda e: e.iota(idi, pattern=[[1, 128]], base=0, channel_multiplier=-1),
                      writes=[("mmean",)])
                S.add("dve", lambda e: e.tensor_copy(idf, idi), reads=[("mmean",)], writes=[("mvar",)])
                S.add("dve", lambda e: e.tensor_single_scalar(idf, idf, 0.0, op=ALU.is_equal),
                      reads=[("mvar",)], writes=[("mvar",)])
                S.add("dve", lambda e: e.tensor_copy(ident_bf[:], idf), reads=[("mvar",)], writes=[("ident",)])
                rms_stats(0)
                rms_apply(0)
            items.append((None, setup))

            def tp(tq):
                e0 = 0 if tq == 0 else HALO + TQ * tq
                ne = 288 if tq == 0 else TQ
                po = HALO if tq == 0 else 0
                aoff = 0 if tq == 0 else HALO
                oe0 = HALO + TQ * tq
                return e0, ne, po, aoff, oe0

            def rms_stats(tq):
                e0, ne, po, aoff, oe0 = tp(tq)
                b = banks(1)[0]
                for k in range(KD):
                    sl = k % 2
                    S.add("act", lambda e, k=k, sl=sl: e.activation(sq[sl][:, 0:ne], x32[:, k, e0:e0 + ne], AF.Square),
                          reads=[("x", k)], writes=[("sqbuf", sl)])
                    S.add("pe", lambda e, k=k, sl=sl: e.matmul(psum[:, b, 0:ne], ones_bf[:], sq[sl][:, 0:ne],
                                                               start=(k == 0), stop=(k == KD - 1)),
                          reads=[("sqbuf", sl), ("ones",)], writes=[pkey(b)])
                S.add("act", lambda e: e.activation(rstd_n[:, 0:ne], psum[:, b, 0:ne], AF.Sqrt, bias=eps_rms, scale=1.0 / D),
                      reads=[pkey(b), ("c0",)], writes=[("rstdn",)])
                S.add("dve", lambda e: e.reciprocal(rstd_n[:, 0:ne], rstd_n[:, 0:ne]), reads=[("rstdn",)], writes=[("rstdn",)])

            def rms_apply(tq):
                e0, ne, po, aoff, oe0 = tp(tq)
                for k in range(KD):
                    S.add("dve", lambda e, k=k: e.scalar_tensor_tensor(
                        out=hbk[k][0][:, 0:ne], in0=x32[:, k, e0:e0 + ne], scalar=cols[:, C_MIX + k:C_MIX + k + 1],
                        in1=rstd_n[:, 0:ne], op0=ALU.mult, op1=ALU.mult),
                        reads=[("x", k), ("rstdn",), ("cols",)], writes=[hbk[k][1]])

            def emit_dg(c):
                sl = c % 2
                wc = C_CONVW + c * CW
                dgv = dg[sl].rearrange("p (j m) -> p j m", j=CW)
                S.add("dve", lambda e: e.tensor_tensor(
                    out=dgv, in0=ident_bf[:].unsqueeze(1).to_broadcast([128, CW, 128]),
                    in1=cols[:, wc:wc + CW].unsqueeze(2).to_broadcast([128, CW, 128]), op=ALU.mult),
                    reads=[("ident",), ("cols",)], writes=[("dg", sl)])

            def emit_conv(tq, c):
                sl = c % 2
                cvk = cvk2[tq % 2]
                dgv = dg[sl].rearrange("p (j m) -> p j m", j=CW)
                b = banks(1)[0]

                def mm(e):
                    ins = None
                    for j in range(CW):
                        ins = e.matmul(psum[:, b, 0:TQ], dgv[:, j, :], ak[c][0][:, 2 + j:2 + j + TQ],
                                       start=(j == 0), stop=(j == CW - 1))
                    return ins
                S.add("pe", mm, reads=[("dg", sl), ak[c][1]], writes=[pkey(b)])
                S.add("act", lambda e: e.activation(cvk[c][0], psum[:, b, 0:TQ], AF.Identity,
                                                    bias=cols[:, C_CONVB + c:C_CONVB + c + 1], scale=1.0),
                      reads=[pkey(b), ("cols",)], writes=[cvk[c][1]])

            sgc = [0]

            def conv_block_item(tq, i, after=None):
                e0, ne, po, aoff, oe0 = tp(tq)
                tl = [(0, ne)]

                def wconv(ws, wk):
                    if i == 0 and tq > 0:
                        S.add("act", lambda e: e.activation(a3[:, :, 0:HALO], a3[:, :, TQ:TQ + HALO], AF.Copy),
                              reads=[k for _, k in ak], writes=[k for _, k in ak])
                    if i >= 1:
                        emit_dg(2 * i - 2)
                        emit_dg(2 * i - 1)
                    for jj in range(2):
                        c = 2 * i + jj
                        bv = banks(1)
                        bg = banks(1)
                        mm_group(ws, wk, KD, 512, jj * 256, hbk, tl, bv)
                        mm_group(ws, wk, KD, 512, jj * 256 + 128, hbk, tl, bg)
                        sl = sgc[0] % 2
                        sgc[0] += 1
                        S.add("act", lambda e, bg=bg, sl=sl: e.activation(sg[sl][:, 0:ne], psum[:, bg[0], 0:ne], AF.Sigmoid),
                              reads=[pkey(bg[0])], writes=[("msg", sl)])
                        S.add("dve", lambda e, bv=bv, sl=sl, c=c: e.tensor_tensor(
                            out=ak[c][0][:, aoff:aoff + ne], in0=sg[sl][:, 0:ne], in1=psum[:, bv[0], 0:ne], op=ALU.mult),
                            reads=[("msg", sl), pkey(bv[0])], writes=[ak[c][1]])
                    if i >= 1:
                        emit_conv(tq, 2 * i - 2)
                        emit_conv(tq, 2 * i - 1)
                    if after is not None:
                        after()
                items.append((B_MIX + i, wconv))

            def u_block_item(tq, ub, after=None):
                e0, ne, po, aoff, oe0 = tp(tq)
                tl = [(0, ne)]
                guk = guk2[tq % 2]

                def wu(ws, wk):
                    if ub == 0:
                        emit_dg(6)
                        emit_dg(7)
                    for mmi in range(4):
                        c = ub * 4 + mmi
                        bu = banks(1)
                        mm_group(ws, wk, KD, 512, mmi * 128, hbk, tl, bu)
                        S.add("act", lambda e, bu=bu, c=c: e.activation(guk[c][0], psum[:, bu[0], po:po + TQ], AF.Gelu),
                              reads=[pkey(bu[0])], writes=[guk[c][1]])
                    if ub == 0:
                        emit_conv(tq, 6)
                        emit_conv(tq, 7)
                    if after is not None:
                        after()
                items.append((B_MIX + 4 + ub, wu))

            def v_block_item(tq, vb, after=None):
                e0, ne, po, aoff, oe0 = tp(tq)

                def wvb(ws, wk):
                    for s_ in range(2):
                        b = banks(1)[0]

                        def mm(e, s_=s_, b=b):
                            ins = None
                            for k in range(KD):
                                ins = e.matmul(psum[:, b, 0:512], hbk[k][0][:, po + s_ * 128:po + (s_ + 1) * 128],
                                               ws[:, k * 512:(k + 1) * 512], start=(k == 0), stop=(k == KD - 1))
                            return ins
                        S.add("pe", mm, reads=[wk] + [k for _, k in hbk], writes=[pkey(b)])
                        S.add("act", lambda e, s_=s_, b=b: e.activation(
                            vtmp[s_][:, vb * 512:(vb + 1) * 512], psum[:, b, 0:512], AF.Gelu,
                            accum_out=stat[:, s_ * 2 + vb:s_ * 2 + vb + 1]),
                            reads=[pkey(b)], writes=[("vt", s_, vb), ("vsum", s_, vb)])
                    if after is not None:
                        after()
                items.append((B_MIX + 6 + vb, wvb))

            def mix_out_item(tq, cb):
                e0, ne, po, aoff, oe0 = tp(tq)

                def wmo(ws, wk):
                    for mmi in range(4):
                        m = cb * 4 + mmi
                        b = banks(1)
                        mm_group(ws, wk, KD, 512, mmi * 128, yk, [(0, TQ)], b)
                        xa = x32[:, m, oe0:oe0 + TQ]
                        S.add("dve", lambda e, b=b, xa=xa: e.tensor_tensor(out=xa, in0=xa, in1=psum[:, b[0], 0:TQ], op=ALU.add),
                              reads=[pkey(b[0]), ("x", m)], writes=[("x", m)])
                items.append((B_MIX + 8 + cb, wmo))

            def chain_segments(tq):
                cvk = cvk2[tq % 2]
                guk = guk2[tq % 2]
                mean, var, tmp = stt[0], stt[1], stt[2]

                def seg_vln():
                    for s_ in range(2):
                        vt = vtmp[s_]
                        vk = [("vt", s_, 0), ("vt", s_, 1)]
                        c_sum, c_sq, c_mean, c_var, c_nm = 8 + s_ * 8, 9 + s_ * 8, 10 + s_ * 8, 11 + s_ * 8, 12 + s_ * 8
                        S.add("act", lambda e, vt=vt, c_sq=c_sq, s_=s_: e.activation(vn[s_], vt, AF.Square, accum_out=stat[:, c_sq:c_sq + 1]),
                              reads=vk, writes=[("vn", s_), ("vst", s_, 0)])
                        S.add("dve", lambda e, s_=s_, c_sum=c_sum: e.tensor_tensor(
                            out=stat[:, c_sum:c_sum + 1], in0=stat[:, s_ * 2:s_ * 2 + 1], in1=stat[:, s_ * 2 + 1:s_ * 2 + 2], op=ALU.add),
                            reads=[("vsum", s_, 0), ("vsum", s_, 1)], writes=[("vst", s_, 1)])
                        S.add("dve", lambda e, c_sum=c_sum, c_mean=c_mean: e.tensor_scalar(
                            stat[:, c_mean:c_mean + 1], stat[:, c_sum:c_sum + 1], 1.0 / 1024, None, op0=ALU.mult),
                            reads=[("vst", s_, 1)], writes=[("vst", s_, 2)])
                        S.add("dve", lambda e, c_mean=c_mean, c_nm=c_nm: e.tensor_tensor(
                            out=stat[:, c_nm:c_nm + 1], in0=stat[:, c_mean:c_mean + 1], in1=stat[:, c_mean:c_mean + 1], op=ALU.mult),
                            reads=[("vst", s_, 2)], writes=[("vst", s_, 3)])
                        S.add("dve", lambda e, c_sq=c_sq, c_nm=c_nm, c_var=c_var: e.scalar_tensor_tensor(
                            out=stat[:, c_var:c_var + 1], in0=stat[:, c_sq:c_sq + 1], scalar=1.0 / 1024,
                            in1=stat[:, c_nm:c_nm + 1], op0=ALU.mult, op1=ALU.subtract),
                            reads=[("vst", s_, 0), ("vst", s_, 3)], writes=[("vst", s_, 4)])
                        S.add("act", lambda e, c_var=c_var: e.activation(stat[:, c_var:c_var + 1], stat[:, c_var:c_var + 1],
                                                                         AF.Sqrt, bias=eps_ln, scale=1.0),
                              reads=[("vst", s_, 4), ("c1",)], writes=[("vst", s_, 4)])
                        S.add("dve", lambda e, c_var=c_var: e.reciprocal(stat[:, c_var:c_var + 1], stat[:, c_var:c_var + 1]),
                              reads=[("vst", s_, 4)], writes=[("vst", s_, 4)])
                        S.add("dve", lambda e, vt=vt, c_mean=c_mean, c_var=c_var: e.tensor_scalar(
                            vt, vt, stat[:, c_mean:c_mean + 1], stat[:, c_var:c_var + 1], op0=ALU.subtract, op1=ALU.mult),
                            reads=vk + [("vst", s_, 2), ("vst", s_, 4)], writes=vk)
                        S.add("dve", lambda e, vt=vt: e.tensor_tensor(out=vt, in0=vt, in1=rows[:, 0:1024], op=ALU.mult),
                              reads=vk + [("rows",)], writes=vk)
                        S.add("dve", lambda e, vt=vt, s_=s_: e.tensor_tensor(out=vn[s_], in0=vt, in1=rows[:, 1024:2048], op=ALU.add),
                              reads=vk + [("rows",)], writes=[("vn", s_)])

                def seg_cstats():
                    b1 = banks(1)[0]
                    b2 = banks(1)[0]
                    for c in range(8):
                        sl = c % 2
                        S.add("act", lambda e, c=c, sl=sl: e.activation(sq[sl][:, 0:TQ], cvk[c][0], AF.Copy),
                              reads=[cvk[c][1]], writes=[("sqbuf", sl)])
                        S.add("pe", lambda e, c=c, sl=sl: e.matmul(psum[:, b1, 0:TQ], ones_bf[:], sq[sl][:, 0:TQ],
                                                                   start=(c == 0), stop=(c == 7)),
                              reads=[("sqbuf", sl), ("ones",)], writes=[pkey(b1)])
                    for c in range(8):
                        sl = c % 2
                        S.add("act", lambda e, c=c, sl=sl: e.activation(sq[sl][:, 0:TQ], cvk[c][0], AF.Square),
                              reads=[cvk[c][1]], writes=[("sqbuf", sl)])
                        S.add("pe", lambda e, c=c, sl=sl: e.matmul(psum[:, b2, 0:TQ], ones_bf[:], sq[sl][:, 0:TQ],
                                                                   start=(c == 0), stop=(c == 7)),
                              reads=[("sqbuf", sl), ("ones",)], writes=[pkey(b2)])
                    S.add("dve", lambda e: e.tensor_scalar(mean, psum[:, b1, 0:TQ], 1.0 / 1024, None, op0=ALU.mult),
                          reads=[pkey(b1)], writes=[("mmean",)])
                    S.add("dve", lambda e: e.tensor_tensor(out=var, in0=mean, in1=mean, op=ALU.mult),
                          reads=[("mmean",)], writes=[("mvar",)])
                    S.add("dve", lambda e: e.scalar_tensor_tensor(out=var, in0=psum[:, b2, 0:TQ], scalar=1.0 / 1024, in1=var,
                                                                  op0=ALU.mult, op1=ALU.subtract),
                          reads=[pkey(b2), ("mvar",)], writes=[("mvar",)])
                    S.add("act", lambda e: e.activation(var, var, AF.Sqrt, bias=eps_ln, scale=1.0),
                          reads=[("mvar",), ("c1",)], writes=[("mvar",)])
                    S.add("dve", lambda e: e.reciprocal(var, var), reads=[("mvar",)], writes=[("mvar",)])

                def seg_capply():
                    for c in range(8):
                        cvc, ck = cvk[c]
                        S.add("dve", lambda e, cvc=cvc: e.tensor_tensor(out=cvc, in0=cvc, in1=mean, op=ALU.subtract),
                              reads=[ck, ("mmean",)], writes=[ck])
                        S.add("dve", lambda e, cvc=cvc, c=c: e.scalar_tensor_tensor(
                            out=cvc, in0=cvc, scalar=cols[:, C_CLNG + c:C_CLNG + c + 1], in1=var, op0=ALU.mult, op1=ALU.mult),
                            reads=[ck, ("mvar",), ("cols",)], writes=[ck])
                        S.add("act", lambda e, cvc=cvc, c=c: e.activation(cvc, cvc, AF.Silu,
                                                                          bias=cols[:, C_CLNB + c:C_CLNB + c + 1], scale=1.0),
                              reads=[ck, ("cols",)], writes=[ck])

                def seg_rmsa():
                    rms_feat("mra", cvk, C_ONC, [(yk[c][0], yk[c][1]) for c in range(8)], TQ, [(0, TQ)], 1024, sq, rstd)

                def seg_sgu():
                    for h in range(8):
                        b = banks(1)[0]

                        def sgm(e, h=h, b=b):
                            ins = None
                            for s_ in range(2):
                                ins = e.matmul(psum[:, b, s_ * 128:(s_ + 1) * 128], vn[s_][:, h * 128:(h + 1) * 128],
                                               wmT[:, h, :], start=True, stop=True)
                            return ins
                        S.add("pe", sgm, reads=[("vn", 0), ("vn", 1), ("wmT",)], writes=[pkey(b)])
                        bsb = rows[:, 2048 + h * 128:2048 + (h + 1) * 128]
                        S.add("dve", lambda e, b=b, bsb=bsb: e.tensor_tensor(
                            out=tmp.rearrange("p (s i) -> p s i", s=2), in0=psum[:, b, 0:TQ].rearrange("p (s i) -> p s i", s=2),
                            in1=bsb.unsqueeze(1).to_broadcast([128, 2, 128]), op=ALU.add),
                            reads=[pkey(b), ("rows",)], writes=[("mtmp", 0)])
                        S.add("dve", lambda e, h=h: e.tensor_tensor(out=guk[h][0], in0=guk[h][0], in1=tmp, op=ALU.mult),
                              reads=[guk[h][1], ("mtmp", 0)], writes=[guk[h][1]])

                def seg_rmsb():
                    rms_feat("mrb", guk, C_ONS, [(yk[8 + c][0], yk[8 + c][1]) for c in range(8)], TQ, [(0, TQ)], 1024, sq, rstd)

                return [seg_cstats, seg_capply, seg_rmsa, seg_vln, seg_sgu, seg_rmsb]

            for i in range(4):
                conv_block_item(0, i)
            for ub in range(2):
                u_block_item(0, ub)
            v_block_item(0, 0, after=(lambda: rms_stats(1)))
            v_block_item(0, 1)
            for tq in range(NT_MIX):
                segs = chain_segments(tq)
                if tq + 1 < NT_MIX:
                    items.append((None, lambda ws, wk, tq=tq: rms_apply(tq + 1)))
                    for i in range(4):
                        conv_block_item(tq + 1, i, after=segs[i])
                    for ub in range(2):
                        u_block_item(tq + 1, ub, after=segs[4 + ub])
                else:
                    items.append((None, lambda ws, wk, segs=segs: [s_() for s_ in segs]))
                for cb in range(4):
                    mix_out_item(tq, cb)
                if tq + 1 < NT_MIX:
                    nxt2 = (lambda tq=tq: rms_stats(tq + 2)) if tq + 2 < NT_MIX else None
                    v_block_item(tq + 1, 0, after=nxt2)
                    v_block_item(tq + 1, 1)

        if stop_after != "f1":
            mixer_phase()

        def attn_phase():
            ar = Arena()
            qb_ = ar.bf16(KD * TOK)
            hb = ar.bf16(KD * TOK)
            kT = ar.bf16(KD * NMEM)
            vv = ar.bf16(2 * D)
            pT = ar.bf16(2 * TOK)
            e32 = [ar.f32(NMEM) for _ in range(2)]
            pb = [ar.bf16(NMEM) for _ in range(2)]
            sq = [ar.bf16(TOK) for _ in range(2)]
            rstd = ar.f32(TOK)
            rstd_m = ar.f32(NMEM)
            m32 = arena[:, 0:KD * NMEM]
            mn = arena[:, KD * NMEM:KD * NMEM + KD * NMEM // 2].bitcast(BF16)
            qk = [(qb_[:, c * TOK:(c + 1) * TOK], ("aq", c)) for c in range(KD)]
            hbk = [(hb[:, k * TOK:(k + 1) * TOK], ("ahb", k)) for k in range(KD)]
            kTk = [(kT[:, c * NMEM:(c + 1) * NMEM], ("akT", c)) for c in range(KD)]
            m32k = [(m32[:, k * NMEM:(k + 1) * NMEM], ("am32", k)) for k in range(KD)]
            mnk = [(mn[:, k * NMEM:(k + 1) * NMEM], ("amn", k)) for k in range(KD)]
            scale = 512.0 ** -0.5
            memT_v = memT_d.rearrange("(k p) m -> p k m", p=128)

            def pre(ws, wk):
                S.barrier()
                for k in range(KD):
                    S.add("sp", lambda e, k=k: e.dma_start(out=m32k[k][0], in_=memT_v[:, k, :]),
                          writes=[m32k[k][1]], dma_sem=f"min{k}")
                rms_split(m32k, C_MEM, mnk, NMEM, [(0, NMEM)], D, sq, rstd_m, rkey=("rstdm",))
            items.append((None, pre))
            cpc = [0]

            def evac(dst, src, reads, writes):
                eng = "act" if cpc[0] % 2 == 0 else "dve"
                cpc[0] += 1
                if eng == "act":
                    S.add("act", lambda e: e.activation(dst, src, AF.Copy), reads=reads, writes=writes)
                else:
                    S.add("dve", lambda e: e.tensor_copy(dst, src), reads=reads, writes=writes)

            for kb in range(4):
                def wkb(ws, wk, kb=kb):
                    for mmi in range(4):
                        c = kb * 4 + mmi
                        b = banks(1)
                        mm_group(ws, wk, KD, 512, mmi * 128, mnk, [(0, NMEM)], b)
                        S.add("dve", lambda e, c=c, b=b: e.tensor_tensor(out=kTk[c][0], in0=psum[:, b[0], 0:NMEM],
                                                                         in1=rstd_m[:, 0:NMEM], op=ALU.mult),
                              reads=[pkey(b[0]), ("rstdm",)], writes=[kTk[c][1]])
                items.append((B_ATT + kb, wkb))
            for vb in range(4):
                def wvb(ws, wk, vb=vb):
                    for mc in range(2):
                        b = banks(1)[0]

                        def mm(e, mc=mc, b=b):
                            ins = None
                            for k in range(KD):
                                ins = e.matmul(psum[:, b, 0:512], mnk[k][0][:, mc * 128:(mc + 1) * 128],
                                               ws[:, k * 512:(k + 1) * 512], start=(k == 0), stop=(k == KD - 1))
                            return ins
                        S.add("pe", mm, reads=[wk] + [k for _, k in mnk], writes=[pkey(b)])
                        evac(vv[:, mc * D + vb * 512:mc * D + (vb + 1) * 512], psum[:, b, 0:512], [pkey(b)], [("av", mc, vb)])
                items.append((B_ATT + 4 + vb, wvb))

            def pre2(ws, wk):
                S.barrier()
                xs = [(x32[:, k, HALO:TE], ("x", k)) for k in range(KD)]
                rms_split(xs, C_XATT, hbk, TOK, tiles_o2, D, sq, rstd)
            items.append((None, pre2))
            for qb in range(4):
                def wqb(ws, wk, qb=qb):
                    for mmi in range(4):
                        c = qb * 4 + mmi
                        bs = banks(2)
                        mm_group(ws, wk, KD, 512, mmi * 128, hbk, tiles_o2, bs)
                        for ti, (t0, nn) in enumerate(tiles_o2):
                            S.add("dve", lambda e, c=c, b=bs[ti], t0=t0, nn=nn: e.scalar_tensor_tensor(
                                out=qk[c][0][:, t0:t0 + nn], in0=psum[:, b, 0:nn], scalar=-scale, in1=rstd[:, t0:t0 + nn],
                                op0=ALU.mult, op1=ALU.mult),
                                reads=[pkey(bs[ti]), ("rstdbuf",)], writes=[("aq", c, ti)])
                items.append((B_ATT + 8 + qb, wqb))

            def core(ws, wk):
                vkeys = [("av", mc, vb) for mc in range(2) for vb in range(4)]
                sbank = {}

                def emit_sc(it):
                    hh, tt = divmod(it, 8)
                    b = banks(1)[0]
                    sbank[it] = b
                    ti = tt // 4

                    def sc(e):
                        ins = None
                        for i in range(4):
                            c = 4 * hh + i
                            ins = e.matmul(psum[:, b, 0:NMEM], qk[c][0][:, tt * 128:(tt + 1) * 128], kTk[c][0],
                                           start=(i == 0), stop=(i == 3))
                        return ins
                    S.add("pe", sc, reads=[("aq", 4 * hh + i, ti) for i in range(4)] + [kTk[4 * hh + i][1] for i in range(4)],
                          writes=[pkey(b)])

                def emit_chain(it):
                    sl = it % 2
                    b = sbank[it]
                    cm = 16 + sl * 4
                    S.add("dve", lambda e: e.tensor_reduce(stat[:, cm + 1:cm + 2], psum[:, b, 0:NMEM], axis=AX.X, op=ALU.min),
                          reads=[pkey(b)], writes=[("as", sl, 1)])
                    S.add("act", lambda e: e.activation(
                        e32[sl], psum[:, b, 0:NMEM], AF.Exp, bias=stat[:, cm + 1:cm + 2], scale=-1.0,
                        accum_out=stat[:, cm + 2:cm + 3]),
                        reads=[pkey(b), ("as", sl, 1)], writes=[("ae", sl), ("as", sl, 2)])
                    S.add("dve", lambda e: e.reciprocal(stat[:, cm + 3:cm + 4], stat[:, cm + 2:cm + 3]),
                          reads=[("as", sl, 2)], writes=[("as", sl, 3)])
                    S.add("dve", lambda e: e.scalar_tensor_tensor(out=pb[sl], in0=e32[sl], scalar=stat[:, cm + 3:cm + 4],
                                                                  in1=rstd_m[:, 0:NMEM], op0=ALU.mult, op1=ALU.mult),
                          reads=[("ae", sl), ("as", sl, 3), ("rstdm",)], writes=[("ap", sl)])

                def emit_tr(it):
                    hh, tt = divmod(it, 8)
                    sl = it % 2
                    bt = banks(1)[0]
                    ptv = psum[:, bt, 0:128].bitcast(BF16)

                    def tr(e):
                        ins = None
                        for mc in range(2):
                            ins = e.transpose(ptv[:, mc * 128:(mc + 1) * 128], pb[sl][:, mc * 128:(mc + 1) * 128], ident_bf[:])
                        return ins
                    S.add("pe", tr, reads=[("ap", sl), ("ident",)], writes=[pkey(bt)])
                    dst = pT.rearrange("p (m t) -> p m t", m=2)[:, :, tt * 128:(tt + 1) * 128]
                    evac(dst, ptv.rearrange("p (m t) -> p m t", m=2), [pkey(bt)], [("apT", tt)])

                def emit_pv(hh):
                    for i in range(4):
                        c = 4 * hh + i
                        bs = banks(2)

                        def pv(e, c=c, bs=bs):
                            ins = None
                            for mc in range(2):
                                for ti, (t0, nn) in enumerate(tiles_o2):
                                    ins = e.matmul(psum[:, bs[ti], 0:nn], vv[:, mc * D + c * 128:mc * D + (c + 1) * 128],
                                                   pT[:, mc * TOK + t0:mc * TOK + t0 + nn], start=(mc == 0), stop=(mc == 1))
                            return ins
                        S.add("pe", pv, reads=vkeys + [("apT", tt) for tt in range(8)], writes=[pkey(b) for b in bs])
                        for ti, (t0, nn) in enumerate(tiles_o2):
                            evac(hbk[c][0][:, t0:t0 + nn], psum[:, bs[ti], 0:nn], [pkey(bs[ti])] + [hbk[c][1]], [hbk[c][1]])

                NIT = 32
                emit_sc(0)
                emit_chain(0)
                for it in range(1, NIT + 1):
                    if it < NIT:
                        emit_sc(it)
                        emit_chain(it)
                    emit_tr(it - 1)
                    if it % 8 == 0:
                        emit_pv(it // 8 - 1)
            items.append((None, core))
            for ob in range(4):
                def wob(ws, wk, ob=ob):
                    for mmi in range(4):
                        m = ob * 4 + mmi
                        bs = banks(2)
                        mm_group(ws, wk, KD, 512, mmi * 128, hbk, tiles_o2, bs)

                        def acc(e, bs=bs, m=m):
                            ins = None
                            for ti, (t0, nn) in enumerate(tiles_o2):
                                xa = x32[:, m, HALO + t0:HALO + t0 + nn]
                                ins = e.tensor_tensor(out=xa, in0=xa, in1=psum[:, bs[ti], 0:nn], op=ALU.add)
                            return ins
                        S.add("dve", acc, reads=[pkey(b) for b in bs] + [("x", m)], writes=[("x", m)])
                items.append((B_ATT + 12 + ob, wob))

        if stop_after not in ("f1", "mix"):
            attn_phase()
        if stop_after not in ("f1", "mix", "att"):
            ffn_phase("f2", C_FFN2, B_FFN2, HALO, TOK, tiles_o2)

        def final(ws, wk):
            S.barrier()
            ar = Arena()
            sq = [ar.bf16(TOK) for _ in range(2)]
            rstd = ar.f32(TOK)
            xs = [(x32[:, k, HALO:TE], ("x", k)) for k in range(KD)]
            if stop_after is None:
                rms_feat("fin", xs, C_FINAL, xs, TOK, tiles_o2, D, sq, rstd)
            for k in range(KD):
                S.add("sp", lambda e, k=k: e.dma_start(out=out_d[k * 128:(k + 1) * 128, :], in_=x32[:, k, HALO:TE]),
                      reads=[("x", k)], dma_sem=f"osem{k % 4}")
        items.append((None, final))

        wq = [(i, it[0]) for i, it in enumerate(items) if it[0] is not None]
        loaded = {}
        nxt = [0]

        def prefetch():
            if nxt[0] < len(wq):
                ii, bidx = wq[nxt[0]]
                nxt[0] += 1
                loaded[ii] = wload(int(offs[bidx]), int(lens[bidx]))

        for _ in range(NSLOT):
            prefetch()
        for i, (bidx, fn) in enumerate(items):
            if bidx is None:
                fn(None, None)
            else:
                ws, wk = loaded.pop(i)
                fn(ws, wk)
                prefetch()
        finals = [(f"osem{i}", S.count[f"osem{i}"]) for i in range(4)]
        S.emit(nc, final_waits=finals)
    return nc


def _prep_inputs(inp):
    x = np.asarray(inp["x"], np.float32)
    mem = np.asarray(inp["mem"], np.float32)
    inp = {k: np.asarray(v, np.float32) for k, v in inp.items()}
    wstream, _ = _build_wstream(inp)
    cols = _build_cols(inp)
    rows = np.concatenate([inp["sgu_ln_g"][0], inp["sgu_ln_b"][0], inp["sgu_b"][0].reshape(-1)])[None, :]
    rows = np.ascontiguousarray(rows, np.float32)
    wsT = np.ascontiguousarray(inp["sgu_w"][0].transpose(2, 0, 1).reshape(128, 1024))
    in_maps = []
    for c in range(8):
        b, q = divmod(c, 4)
        s0 = q * TOK
        if q == 0:
            xe = np.concatenate([np.zeros((HALO, D), np.float32), x[b, 0:TOK]], axis=0)
        else:
            xe = x[b, s0 - HALO:s0 + TOK]
        in_maps.append({
            "xT": np.ascontiguousarray(xe.T),
            "memT": np.ascontiguousarray(mem[b].T),
            "wstream": wstream,
            "cols": cols,
            "rows": rows,
            "wsT": wsT,
        })
    return in_maps


def kernel(**inputs):
    stop_after = os.environ.get("MK_STOP") or None
    in_maps = _prep_inputs(inputs)
    nc = build_program(stop_after)
    ncores = int(os.environ.get("MK_CORES", "8"))
    res = run_bass_kernel_spmd(nc, in_maps[:ncores], core_ids=list(range(ncores)))
    out = np.zeros((2, SEQ, D), np.float32)
    for c in range(ncores):
        b, q = divmod(c, 4)
        out[b, q * TOK:(q + 1) * TOK, :] = res.results[c]["outT"].T
    return out
```
